# Optimizing a Trainium2 kernel written in Bass

```python
import math
import jax
import jax.numpy as jnp
from jax import lax
import numpy as np

D_MODEL = 1024
BATCH = 8
SEQ = 4096
DEPTH = 2

N_MIXERS = 2
N_MEM = 256
HEAD_DIM = 64
MIX_HEADS = 12
MEM_HEADS = 4
MIX_WIDTH = MIX_HEADS * HEAD_DIM
MEM_WIDTH = MEM_HEADS * HEAD_DIM
GATE_WIDTH = MIX_WIDTH + MEM_WIDTH
CONV_WIDTH = 4
DELTA_CHUNK = 64
MOBA_BLOCK = 256
MOBA_TOPK = 3
MOBA_QCHUNK = 32
ROPE_THETA = 10000.0
NORM_EPS = 1e-6
MASK_VALUE = -1e30
DELTA_IN = 3 * MIX_WIDTH + GATE_WIDTH + MEM_WIDTH + 2 * MIX_HEADS
MOBA_IN = 3 * MIX_WIDTH + GATE_WIDTH + MEM_WIDTH

kernel_name = 'hybrid_deltanet_moba_memory'


def rmsnorm(x, g):
    xf = x.astype(jnp.float32)
    y = xf * lax.rsqrt(jnp.mean(xf * xf, axis=-1, keepdims=True) + NORM_EPS)
    return (y * g.astype(jnp.float32)).astype(x.dtype)


def l2norm(x):
    xf = x.astype(jnp.float32)
    return xf * lax.rsqrt(jnp.sum(xf * xf, axis=-1, keepdims=True) + NORM_EPS)


def apply_rope(x, positions):
    d = x.shape[-1]
    half = d // 2
    inv_freq = ROPE_THETA ** (-jnp.arange(half, dtype=jnp.float32) * (2.0 / d))
    ang = positions.astype(jnp.float32)[..., None] * inv_freq
    cos = jnp.cos(ang)[:, :, None, :]
    sin = jnp.sin(ang)[:, :, None, :]
    xf = x.astype(jnp.float32)
    x1, x2 = xf[..., :half], xf[..., half:]
    return jnp.concatenate([x1 * cos - x2 * sin, x2 * cos + x1 * sin], axis=-1).astype(x.dtype)


def causal_short_conv(x, w):
    c = x.shape[-1]
    return lax.conv_general_dilated(
        x, w.astype(x.dtype)[:, None, :], window_strides=(1,),
        padding=[(CONV_WIDTH - 1, 0)], dimension_numbers=('NWC', 'WIO', 'NWC'),
        feature_group_count=c)


def gated_delta_rule(q, k, v, log_decay, beta):
    B_, S_, H_, dk = q.shape
    dv = v.shape[-1]
    C = DELTA_CHUNK
    N = S_ // C

    def to_chunks(t):
        t = t.reshape((B_, N, C, H_) + t.shape[3:])
        return jnp.moveaxis(t, (1, 3), (0, 2))

    qc = to_chunks(q * (dk ** -0.5))
    kc = to_chunks(k)
    vc = to_chunks(v)
    gcum = jnp.cumsum(to_chunks(log_decay), axis=-1)
    bc = to_chunks(beta)
    tri_incl = jnp.tril(jnp.ones((C, C), dtype=bool))
    tri_strict = jnp.tril(jnp.ones((C, C), dtype=bool), -1)
    decay = jnp.exp(jnp.where(tri_incl, gcum[..., :, None] - gcum[..., None, :], -jnp.inf))
    kb = kc * bc[..., None]
    lower = jnp.where(tri_strict, jnp.einsum('nbhid,nbhjd->nbhij', kb, kc) * decay, 0.0)
    eye = jnp.eye(C, dtype=jnp.float32)
    rhs = jnp.concatenate([vc * bc[..., None], kb * jnp.exp(gcum)[..., None]], axis=-1)
    sol = lax.linalg.triangular_solve(eye + lower, rhs, left_side=True, lower=True,
                                      unit_diagonal=True)
    u, w = sol[..., :dv], sol[..., dv:]
    attn_intra = jnp.where(tri_incl, jnp.einsum('nbhid,nbhjd->nbhij', qc, kc) * decay, 0.0)

    def step(state, xs):
        q_i, k_i, u_i, w_i, g_i, a_i = xs
        v_new = u_i - jnp.einsum('bhcd,bhde->bhce', w_i, state)
        o = (jnp.einsum('bhcd,bhde->bhce', q_i * jnp.exp(g_i)[..., None], state)
             + jnp.einsum('bhij,bhje->bhie', a_i, v_new))
        g_last = g_i[..., -1:]
        state = (state * jnp.exp(g_last)[..., None]
                 + jnp.einsum('bhcd,bhce->bhde', k_i * jnp.exp(g_last - g_i)[..., None], v_new))
        return state, o

    s0 = jnp.zeros((B_, H_, dk, dv), jnp.float32)
    _, o = lax.scan(step, s0, (qc, kc, u, w, gcum, attn_intra))
    return jnp.moveaxis(o, (0, 2), (1, 3)).reshape(B_, S_, H_, dv)


def moba_attention(q, k, v):
    B_, S_, H_, d = q.shape
    nb = max(-(-S_ // MOBA_BLOCK), MOBA_TOPK)
    sp = nb * MOBA_BLOCK
    pad = ((0, 0), (0, sp - S_), (0, 0), (0, 0))
    kblk = jnp.moveaxis(jnp.pad(k, pad), 2, 1).reshape(B_, H_, nb, MOBA_BLOCK, d)
    vblk = jnp.moveaxis(jnp.pad(v, pad), 2, 1).reshape(B_, H_, nb, MOBA_BLOCK, d)
    kmean = jnp.mean(kblk.astype(jnp.float32), axis=3)
    nq = S_ // MOBA_QCHUNK
    qch = jnp.moveaxis(q, 2, 1).reshape(B_, H_, nq, MOBA_QCHUNK, d).transpose(2, 0, 1, 3, 4)
    scale = d ** -0.5
    bidx = jnp.arange(B_)[:, None, None]
    hidx = jnp.arange(H_)[None, :, None]
    blocks = jnp.arange(nb)

    def one_chunk(args):
        c, qc = args
        qpos = c * MOBA_QCHUNK + jnp.arange(MOBA_QCHUNK)
        own = (c * MOBA_QCHUNK) // MOBA_BLOCK
        gate = jnp.einsum('bhqd,bhnd->bhqn', qc.astype(jnp.float32), kmean)
        gate = jnp.where(blocks < own, gate, -jnp.inf)
        top_val, top_idx = lax.top_k(gate, MOBA_TOPK)
        sel_valid = jnp.isfinite(top_val)
        kown = lax.dynamic_index_in_dim(kblk, own, axis=2, keepdims=False)
        vown = lax.dynamic_index_in_dim(vblk, own, axis=2, keepdims=False)
        kpos = own * MOBA_BLOCK + jnp.arange(MOBA_BLOCK)
        causal = kpos[None, :] <= qpos[:, None]
        logit_own = jnp.einsum('bhqd,bhkd->bhqk', qc, kown).astype(jnp.float32) * scale
        logits = [jnp.where(causal, logit_own, MASK_VALUE)]
        for s in range(MOBA_TOPK):
            ks = kblk[bidx, hidx, top_idx[..., s]]
            ls = jnp.einsum('bhqd,bhqkd->bhqk', qc, ks).astype(jnp.float32) * scale
            logits.append(jnp.where(sel_valid[..., s, None], ls, MASK_VALUE))
        p = jax.nn.softmax(jnp.concatenate(logits, axis=-1), axis=-1).astype(v.dtype)
        out = jnp.einsum('bhqk,bhkd->bhqd', p[..., :MOBA_BLOCK], vown)
        for s in range(MOBA_TOPK):
            vs = vblk[bidx, hidx, top_idx[..., s]]
            ps = p[..., (s + 1) * MOBA_BLOCK:(s + 2) * MOBA_BLOCK]
            out = out + jnp.einsum('bhqk,bhqkd->bhqd', ps, vs)
        return out

    out = lax.map(one_chunk, (jnp.arange(nq), qch))
    return out.transpose(1, 0, 3, 2, 4).reshape(B_, S_, H_, d)


def memory_attention(mq, mem, mem_norm_g, w_mem_kv):
    B_, S_, _ = mq.shape
    kv = rmsnorm(mem, mem_norm_g) @ w_mem_kv
    mk, mv = jnp.split(kv, 2, axis=-1)
    mk = mk.reshape(B_, -1, MEM_HEADS, HEAD_DIM)
    mv = mv.reshape(B_, -1, MEM_HEADS, HEAD_DIM)
    q = mq.reshape(B_, S_, MEM_HEADS, HEAD_DIM)
    logits = jnp.einsum('bshd,bmhd->bhsm', q, mk).astype(jnp.float32) * (HEAD_DIM ** -0.5)
    p = jax.nn.softmax(logits, axis=-1).astype(mv.dtype)
    return jnp.einsum('bhsm,bmhd->bshd', p, mv).reshape(B_, S_, MEM_WIDTH)


def gated_output(mix, memo, z, w_out):
    y = jnp.concatenate([mix, memo.astype(mix.dtype)], axis=-1) * jax.nn.silu(z)
    return y @ w_out


def delta_layer(h, mem, norm_g, w_in, conv_w, a_log, dt_bias, o_norm, mem_norm_g, w_mem_kv, w_out):
    B_, S_, _ = h.shape
    proj = rmsnorm(h, norm_g) @ w_in
    i1 = 3 * MIX_WIDTH
    i2 = i1 + GATE_WIDTH
    i3 = i2 + MEM_WIDTH
    qkv, z, mq, ba = jnp.split(proj, [i1, i2, i3], axis=-1)
    qkv = jax.nn.silu(causal_short_conv(qkv, conv_w))
    q, k, v = [t.reshape(B_, S_, MIX_HEADS, HEAD_DIM) for t in jnp.split(qkv, 3, axis=-1)]
    b_raw, a_raw = jnp.split(ba.astype(jnp.float32), 2, axis=-1)
    beta = jax.nn.sigmoid(b_raw)
    log_decay = -jnp.exp(a_log.astype(jnp.float32)) * jax.nn.softplus(a_raw + dt_bias.astype(jnp.float32))
    o = gated_delta_rule(l2norm(q), l2norm(k), v.astype(jnp.float32), log_decay, beta)
    o = rmsnorm(o, o_norm).reshape(B_, S_, MIX_WIDTH).astype(h.dtype)
    m = memory_attention(mq, mem, mem_norm_g, w_mem_kv)
    return gated_output(o, m, z, w_out)


def moba_layer(h, mem, positions, norm_g, w_in, mem_norm_g, w_mem_kv, w_out):
    B_, S_, _ = h.shape
    proj = rmsnorm(h, norm_g) @ w_in
    q, k, v, z, mq = jnp.split(proj, [MIX_WIDTH, 2 * MIX_WIDTH, 3 * MIX_WIDTH,
                                      3 * MIX_WIDTH + GATE_WIDTH], axis=-1)
    q = apply_rope(q.reshape(B_, S_, MIX_HEADS, HEAD_DIM), positions)
    k = apply_rope(k.reshape(B_, S_, MIX_HEADS, HEAD_DIM), positions)
    v = v.reshape(B_, S_, MIX_HEADS, HEAD_DIM)
    o = moba_attention(q, k, v).reshape(B_, S_, MIX_WIDTH)
    m = memory_attention(mq, mem, mem_norm_g, w_mem_kv)
    return gated_output(o, m, z, w_out)


def setup_inputs(seed: int = 0) -> dict:
    key = jax.random.key(seed)
    ks = jax.random.split(key, 24)
    f32 = jnp.float32

    def dense(k, fan_in, fan_out):
        return jax.random.normal(k, (fan_in, fan_out), f32) * fan_in ** -0.5

    def gain(k, n):
        return 1.0 + 0.05 * jax.random.normal(k, (n,), f32)

    x = jax.random.normal(ks[0], (BATCH, SEQ, D_MODEL), f32)
    mem = jax.random.normal(ks[1], (BATCH, N_MEM, D_MODEL), f32)
    start = jax.random.randint(ks[2], (BATCH, 1), 0, 1024, dtype=jnp.int32)
    positions = start + jnp.arange(SEQ, dtype=jnp.int32)[None, :]
    dt = jnp.exp(jax.random.uniform(ks[7], (MIX_HEADS,), f32,
                                    minval=math.log(1e-3), maxval=math.log(1e-1)))
    return {
        'x': x,
        'mem': mem,
        'positions': positions,
        'norm_0': gain(ks[3], D_MODEL),
        'w_in_0': dense(ks[4], D_MODEL, DELTA_IN),
        'conv_w_0': jax.random.normal(ks[5], (CONV_WIDTH, 3 * MIX_WIDTH), f32) * CONV_WIDTH ** -0.5,
        'a_log_0': jnp.log(jax.random.uniform(ks[6], (MIX_HEADS,), f32, minval=1.0, maxval=16.0)),
        'dt_bias_0': dt + jnp.log(-jnp.expm1(-dt)),
        'o_norm_0': gain(ks[8], HEAD_DIM),
        'mem_norm_0': gain(ks[9], D_MODEL),
        'w_mem_kv_0': dense(ks[10], D_MODEL, 2 * MEM_WIDTH),
        'w_out_0': dense(ks[11], GATE_WIDTH, D_MODEL),
        'norm_1': gain(ks[12], D_MODEL),
        'w_in_1': dense(ks[13], D_MODEL, MOBA_IN),
        'mem_norm_1': gain(ks[14], D_MODEL),
        'w_mem_kv_1': dense(ks[15], D_MODEL, 2 * MEM_WIDTH),
        'w_out_1': dense(ks[16], GATE_WIDTH, D_MODEL),
        'final_norm': gain(ks[17], D_MODEL),
    }


def reference(x, mem, positions, norm_0, w_in_0, conv_w_0, a_log_0, dt_bias_0, o_norm_0,
              mem_norm_0, w_mem_kv_0, w_out_0, norm_1, w_in_1, mem_norm_1, w_mem_kv_1,
              w_out_1, final_norm):
    layers = [
        ('delta', (norm_0, w_in_0, conv_w_0, a_log_0, dt_bias_0, o_norm_0,
                   mem_norm_0, w_mem_kv_0, w_out_0)),
        ('moba', (norm_1, w_in_1, mem_norm_1, w_mem_kv_1, w_out_1)),
    ]
    h = x
    for i in range(DEPTH):
        kind, params = layers[i]
        if i % N_MIXERS == 0:
            h = h + delta_layer(h, mem, *params)
        else:
            h = h + moba_layer(h, mem, positions, *params)
    return rmsnorm(h, final_norm)
```

```python
import numpy as np
from contextlib import ExitStack
import concourse.bass as bass
import concourse.mybir as mybir
from concourse.bass_utils import run_bass_kernel_spmd

F32 = mybir.dt.float32
BF16 = mybir.dt.bfloat16
F32R = mybir.dt.float32r
I32 = mybir.dt.int32
AF = mybir.ActivationFunctionType
ALU = mybir.AluOpType

D = 1024
NMEM = 256
HD = 64
NH = 12
MH = 4
DELTA_IN = 3608
MOBA_IN = 3584
EPS = 1e-6
NEG = -30000.0


class Buf:
    __slots__ = ("w", "r", "psum")

    def __init__(self, psum=False):
        self.w = None
        self.r = {}
        self.psum = psum


class T:
    def __init__(self, t, psum=False):
        self.t = t
        self.b = Buf(psum)

    def __getitem__(self, key):
        return self.t[key]


class Reg:
    def __init__(self, bank, ap):
        self.ap = ap
        self.b = bank.b


class View:
    def __init__(self, ap):
        self.ap = ap
        self.b = Buf()

    def __getitem__(self, key):
        return self.ap[key]


class Ring:
    def __init__(self, regs):
        self.regs = regs
        self.i = 0

    def next(self):
        r = self.regs[self.i % len(self.regs)]
        self.i += 1
        return r


class Eng:
    def __init__(self, name, h, sem):
        self.name = name
        self.h = h
        self.sem = sem
        self.count = 0
        self.known = {}


def _b(x):
    return x if isinstance(x, Buf) else x.b


def run_tasks(factories, nif, stagger=0, pools=None):
    pending = [f if isinstance(f, tuple) else (None, f) for f in factories]
    free = {None: list(range(nif))}
    for name, n in (pools or {}).items():
        free[name] = list(range(n))
    active = []
    since = stagger
    while active or pending:
        if pending and since >= stagger and len(active) < nif:
            for idx, (pool, fac) in enumerate(pending):
                if free[pool]:
                    slot = free[pool].pop(0)
                    pending.pop(idx)
                    active.append((fac(slot), pool, slot))
                    since = 0
                    break
        since += 1
        for item in list(active):
            g, pool, slot = item
            try:
                next(g)
            except StopIteration:
                active.remove(item)
                free[pool].append(slot)


class KB:
    def __init__(self, nc, es, n_dma_sems=12):
        self.nc = nc
        self.es = es
        self.eng = {}
        for name, h in (("pe", nc.tensor), ("act", nc.scalar), ("dve", nc.vector), ("pool", nc.gpsimd), ("sp", nc.sync)):
            sem = es.enter_context(nc.semaphore("sem_" + name))
            self.eng[name] = Eng(name, h, sem)
        self.dsems = [es.enter_context(nc.semaphore("dsem%d" % i)) for i in range(n_dma_sems)]
        self.dvals = [0] * n_dma_sems
        self.di = 0
        self.nid = 0
        self.out_tokens = []
        self.ninstr = 0
        self.alloc_log = []

    def sb(self, name, shape, dt):
        self.nid += 1
        nb = int(np.prod(shape[1:])) * (2 if dt == BF16 else 4)
        self.alloc_log.append((name, nb))
        return T(self.es.enter_context(self.nc.sbuf_tensor("%s_u%d" % (name, self.nid), list(shape), dt)))

    def ps(self, name, shape, dt):
        return T(self.es.enter_context(self.nc.psum_tensor(name, list(shape), dt)), psum=True)

    def _waits(self, E, reads, writes):
        toks = {}

        def add(tok, psum):
            if tok is None:
                return
            s, v = tok
            if s is E.sem and (psum or E.name in ("pe", "sp")):
                return
            key = id(s)
            if key not in toks or toks[key][1] < v:
                toks[key] = (s, v)

        for b in reads:
            bb = _b(b)
            add(bb.w, bb.psum)
            if bb.psum:
                for tok in bb.r.values():
                    add(tok, True)
        for b in writes:
            bb = _b(b)
            add(bb.w, bb.psum)
            for tok in bb.r.values():
                add(tok, bb.psum)
        for key, (s, v) in toks.items():
            if E.known.get(key, 0) >= v:
                continue
            E.h.wait_ge(s, v)
            self.ninstr += 1
            E.known[key] = v

    def _record(self, E, tok, reads, writes):
        for b in reads:
            bb = _b(b)
            if bb.psum:
                bb.w = tok
                bb.r = {}
            else:
                bb.r[id(tok[0])] = tok
        for b in writes:
            bb = _b(b)
            bb.w = tok
            bb.r = {}

    def op(self, eng, fn, r=(), w=()):
        E = self.eng[eng]
        self._waits(E, r, w)
        ins = fn(E.h)
        self.ninstr += 1
        E.count += 1
        ins.then_inc(E.sem, 1)
        tok = (E.sem, E.count)
        self._record(E, tok, r, w)
        return tok

    def dma(self, out, in_, r=(), w=(), queue="sp", is_output=False, **kw):
        E = self.eng[queue]
        self._waits(E, r, w)
        i = self.di % len(self.dsems)
        self.di += 1
        sem = self.dsems[i]
        if self.dvals[i] > 0 and E.known.get(id(sem), 0) < self.dvals[i]:
            E.h.wait_ge(sem, self.dvals[i])
            E.known[id(sem)] = self.dvals[i]
        ins = E.h.dma_start(out=out, in_=in_, **kw)
        self.ninstr += 1
        ins.then_inc(sem, 16)
        self.dvals[i] += 16
        tok = (sem, self.dvals[i])
        self._record(E, tok, r, w)
        if is_output:
            self.out_tokens.append(tok)
        return tok

    def barrier(self):
        toks = [(E.sem, E.count) for E in self.eng.values() if E.count > 0]
        toks += [(s, v) for s, v in zip(self.dsems, self.dvals) if v > 0]
        for E in self.eng.values():
            for s, v in toks:
                if s is E.sem:
                    continue
                if E.known.get(id(s), 0) >= v:
                    continue
                E.h.wait_ge(s, v)
                E.known[id(s)] = v

    def finish(self):
        E = self.eng["sp"]
        for s, v in self.out_tokens:
            if E.known.get(id(s), 0) < v:
                E.h.wait_ge(s, v)
                E.known[id(s)] = v


def build_program(S, do_l0=True, do_l1=True, dbg=False, l1stop=99):
    assert S % 256 == 0
    NT = S // 128
    NCH = S // 256
    nc = bass.Bass("TRN2", target_bir_lowering=False)

    def din(name, shape, dt=F32):
        return nc.dram_tensor(name, list(shape), dt, kind="ExternalInput").ap()

    x_d = din("x", [S, D])
    mem_d = din("mem", [NMEM, D])
    pos_d = din("positions", [S], I32)
    norm0_d = din("norm_0", [D])
    win0_d = din("w_in_0", [D, DELTA_IN])
    conv_d = din("conv_w_0", [4, 2304])
    alog_d = din("a_log_0", [NH])
    dtb_d = din("dt_bias_0", [NH])
    onorm_d = din("o_norm_0", [HD])
    mnorm0_d = din("mem_norm_0", [D])
    wm0_d = din("w_mem_kv_0", [D, 512])
    wout0_d = din("w_out_0", [D, D])
    norm1_d = din("norm_1", [D])
    win1_d = din("w_in_1", [D, MOBA_IN])
    mnorm1_d = din("mem_norm_1", [D])
    wm1_d = din("w_mem_kv_1", [D, 512])
    wout1_d = din("w_out_1", [D, D])
    fnorm_d = din("final_norm", [D])
    out_d = nc.dram_tensor("out", [S, D], F32, kind="ExternalOutput").ap()
    if do_l1:
        h1_d = nc.dram_tensor("h1", [S, D], F32).ap()
    else:
        h1_d = out_d
    w1b_d = nc.dram_tensor("w1b", [128, 8, MOBA_IN], BF16).ap()

    with ExitStack() as es:
        k = KB(nc, es)
        PA = [k.ps("PA%d" % i, [128, 512], F32) for i in range(2)]
        PB = k.ps("PB", [128, 512], F32)
        PTB = k.ps("PTB", [128, 1024], BF16)
        PD = [k.ps("PD%d" % i, [128, 512], F32) for i in range(4)]

        identf = k.sb("identf", [128, 128], F32)
        identb = k.sb("identb", [128, 128], BF16)
        tri = k.sb("tri", [128, 128], F32)
        cmaskb = k.sb("cmaskb", [128, 128], BF16)
        onesf = k.sb("onesf", [128, 256], F32)
        negmask = k.sb("negmask", [128, 256], F32)
        blockones = k.sb("blockones", [128, 128], F32)
        k.op("pool", lambda e: e.memset(identf[:], 1.0), w=[identf])
        k.op("pool", lambda e: e.affine_select(out=identf[:], in_=identf[:], pattern=[[-1, 128]], compare_op=ALU.is_equal,
                                               fill=0.0, base=0, channel_multiplier=1), r=[identf], w=[identf])
        k.op("dve", lambda e: e.tensor_copy(identb[:], identf[:]), r=[identf], w=[identb])
        k.op("pool", lambda e: e.memset(tri[:], 1.0), w=[tri])
        k.op("pool", lambda e: e.affine_select(out=tri[:], in_=tri[:], pattern=[[1, 128]], compare_op=ALU.is_ge,
                                               fill=0.0, base=0, channel_multiplier=-1), r=[tri], w=[tri])
        k.op("dve", lambda e: e.tensor_copy(cmaskb[:], tri[:]), r=[tri], w=[cmaskb])
        k.op("pool", lambda e: e.memset(onesf[:], 1.0), w=[onesf])
        k.op("pool", lambda e: e.memset(negmask[:], 0.0), w=[negmask])
        k.op("pool", lambda e: e.affine_select(out=negmask[:, 0:128], in_=negmask[:, 0:128], pattern=[[1, 128]], compare_op=ALU.is_ge,
                                               fill=NEG, base=0, channel_multiplier=-1), r=[negmask], w=[negmask])
        k.op("pool", lambda e: e.affine_select(out=negmask[:, 128:256], in_=negmask[:, 128:256], pattern=[[1, 128]], compare_op=ALU.is_gt,
                                               fill=NEG, base=0, channel_multiplier=-1), r=[negmask], w=[negmask])
        identr = k.sb("identr", [128, 128], F32R)
        negmaskr = k.sb("negmaskr", [128, 256], F32R)
        k.op("dve", lambda e: e.tensor_copy(identr[:], identf[:]), r=[identf], w=[identr])
        k.op("dve", lambda e: e.tensor_copy(negmaskr[:], negmask[:]), r=[negmask], w=[negmaskr])
        k.op("pool", lambda e: e.memset(blockones[:], 0.0), w=[blockones])
        k.op("pool", lambda e: e.memset(blockones[0:64, 0:64], 1.0), r=[blockones], w=[blockones])
        k.op("pool", lambda e: e.memset(blockones[64:128, 64:128], 1.0), r=[blockones], w=[blockones])

        r128 = Ring([Reg(PD[0], PD[0][:, j * 128:(j + 1) * 128]) for j in range(4)])
        r256 = Ring([Reg(bk, bk[:, j * 256:(j + 1) * 256]) for j in range(2) for bk in (PD[1], PD[2], PD[3], PB)])
        rproj = Ring([Reg(PA[i], PA[i][:, 0:256]) for i in range(2)])
        rbank = Ring([Reg(PA[i], PA[i][:, :]) for i in range(2)])

        gstage = k.sb("gstage", [32, 128], F32)
        for i, g in enumerate((norm0_d, mnorm0_d, norm1_d, mnorm1_d)):
            k.dma(gstage[i * 8:(i + 1) * 8, :], g.rearrange("(k p) -> k p", p=128), w=[gstage])
        gcols = k.sb("gcols", [128, 32], F32)
        rg = r128.next()
        k.op("pe", lambda e: e.transpose(rg.ap[:, 0:32], gstage[0:32, :], identf[0:32, 0:32]), r=[gstage, identf], w=[rg])
        k.op("dve", lambda e: e.tensor_copy(gcols[:], rg.ap[:, 0:32]), r=[rg], w=[gcols])
        cstage = k.sb("cstage", [72, 128], F32)
        k.dma(cstage[:], conv_d.rearrange("t (f p) -> (t f) p", p=128), w=[cstage])
        cw = k.sb("cw", [128, 72], F32)
        rg2 = r128.next()
        k.op("pe", lambda e: e.transpose(rg2.ap[:, 0:72], cstage[0:72, :], identf[0:72, 0:72]), r=[cstage, identf], w=[rg2])
        k.op("dve", lambda e: e.tensor_copy(cw[:], rg2.ap[:, 0:72]), r=[rg2], w=[cw])
        alog_bc = k.sb("alog_bc", [128, NH], F32)
        dtb_bc = k.sb("dtb_bc", [128, NH], F32)
        onorm_bc = k.sb("onorm_bc", [128, HD], F32)
        k.dma(alog_bc[:], alog_d.partition_broadcast(128), w=[alog_bc])
        k.dma(dtb_bc[:], dtb_d.partition_broadcast(128), w=[dtb_bc])
        k.dma(onorm_bc[:], onorm_d.partition_broadcast(128), w=[onorm_bc])
        nA_bc = k.sb("nA_bc", [128, NH], F32)
        k.op("act", lambda e: e.activation(out=nA_bc[:], in_=alog_bc[:], func=AF.Exp), r=[alog_bc], w=[nA_bc])
        k.op("dve", lambda e: e.tensor_scalar(out=nA_bc[:], in0=nA_bc[:], scalar1=-1.0, scalar2=None, op0=ALU.mult), r=[nA_bc], w=[nA_bc])

        junk = {}
        xs_ring = [k.sb("xs%d" % i, [128, D], BF16) for i in range(1)]
        xs_i = [0]
        small = {}

        def smalltile(name, shape, dt=F32, n=2):
            key = name
            if key not in small:
                small[key] = [[k.sb("%s_%d" % (name, i), shape, dt) for i in range(n)], 0]
            lst = small[key]
            t = lst[0][lst[1] % n]
            lst[1] += 1
            return t

        def norm_transpose(src, src_bufs, dstT, col0):
            ss = smalltile("nt_ss", [128, 1])
            k.op("act", lambda e: e.activation(out=junk["ap"], in_=src, func=AF.Square, accum_out=ss[:]), r=src_bufs, w=junk["bufs"] + [ss])
            k.op("act", lambda e: e.activation(out=ss[:], in_=ss[:], func=AF.Ln, scale=1.0 / D, bias=EPS), r=[ss], w=[ss])
            k.op("act", lambda e: e.activation(out=ss[:], in_=ss[:], func=AF.Exp, scale=-0.5), r=[ss], w=[ss])
            xs = xs_ring[0]
            xs_i[0] += 1
            k.op("act", lambda e: e.activation(out=xs[:], in_=src, func=AF.Copy, scale=ss[:]), r=list(src_bufs) + [ss], w=[xs])

            def tr(e):
                ins = None
                for kc in range(8):
                    ins = e.transpose(PTB[:, kc * 128:(kc + 1) * 128], xs[:, kc * 128:(kc + 1) * 128], identb[:])
                return ins
            k.op("pe", tr, r=[xs, identb], w=[PTB])
            k.op("dve", lambda e: e.tensor_copy(dstT[:, :, col0:col0 + 128], PTB[:].rearrange("p (k t) -> p k t", k=8)), r=[PTB], w=[dstT])

        ws_i = [0]
        wst = {}

        def alloc_wstage():
            wst["t"] = [k.sb("wstage%d" % i, [128, 1024], F32) for i in range(2)]

        def prep_weight(w_d, ncols, gcol0, dst_fn, dst_bufs, engs=("act", "dve"), post=None):
            for kc in range(8):
                for c0 in range(0, ncols, 1024):
                    n = min(1024, ncols - c0)
                    st = wst["t"][ws_i[0] % 2]
                    eng = engs[ws_i[0] % len(engs)]
                    ws_i[0] += 1
                    k.dma(st[:, 0:n], w_d[kc * 128:(kc + 1) * 128, c0:c0 + n], w=[st])
                    dst = dst_fn(kc, c0, n)
                    if gcol0 is None:
                        if eng == "act":
                            k.op("act", lambda e: e.copy(dst, st[:, 0:n]), r=[st], w=dst_bufs)
                        else:
                            k.op(eng, lambda e: e.tensor_copy(dst, st[:, 0:n]), r=[st], w=dst_bufs)
                    else:
                        gc = gcols[:, gcol0 + kc:gcol0 + kc + 1]
                        if eng == "act":
                            k.op("act", lambda e: e.activation(out=dst, in_=st[:, 0:n], func=AF.Copy, scale=gc), r=[st, gcols], w=dst_bufs)
                        else:
                            k.op(eng, lambda e: e.tensor_scalar(out=dst, in0=st[:, 0:n], scalar1=gc, scalar2=None, op0=ALU.mult), r=[st, gcols], w=dst_bufs)
                    if post is not None:
                        post(kc, c0, n)

        memT = k.sb("memT", [128, 8, NMEM], BF16)
        mkT = [k.sb("mkT%d" % l, [128, 2, NMEM], BF16) for l in range(2)]
        mva = [k.sb("mva%d" % l, [128, 2, MH, HD + 1], BF16) for l in range(2)]
        es_s = ExitStack()
        k.es = es_s
        alloc_wstage()
        memtok = k.sb("memtok", [128, 2, D], F32)
        sj_ = k.sb("sqjunk_s", [128, D], BF16)
        junk["ap"], junk["bufs"] = sj_[:], [sj_]
        for mt in range(2):
            k.dma(memtok[:, mt, :], mem_d[mt * 128:(mt + 1) * 128, :], w=[memtok])
        for mt in range(2):
            norm_transpose(memtok[:, mt, :], [memtok], memT, mt * 128)
        wmb = k.sb("wmb", [128, 8, 512], BF16)
        for l, (wm_d, gc0) in enumerate(((wm0_d, 8), (wm1_d, 24))):
            prep_weight(wm_d, 512, gc0, lambda kc, c0, n: wmb[:, kc, c0:c0 + n], [wmb])
            for ft in range(2):
                rr = rproj.next()

                def mm(e, ft=ft, rr=rr):
                    ins = None
                    for kc in range(8):
                        ins = e.matmul(rr.ap, wmb[:, kc, ft * 128:(ft + 1) * 128], memT[:, kc, :], start=(kc == 0), stop=(kc == 7))
                    return ins
                k.op("pe", mm, r=[wmb, memT], w=[rr])
                k.op("act", lambda e, ft=ft, rr=rr: e.copy(mkT[l][:, ft, :], rr.ap), r=[rr], w=[mkT[l]])
            k.op("pool", lambda e: e.memset(mva[l][:], 1.0), w=[mva[l]])
            for mt in range(2):
                rr = rproj.next()

                def mm2(e, mt=mt, rr=rr):
                    ins = None
                    for kc in range(8):
                        ins = e.matmul(rr.ap, memT[:, kc, mt * 128:(mt + 1) * 128], wmb[:, kc, 256:512], start=(kc == 0), stop=(kc == 7))
                    return ins
                k.op("pe", mm2, r=[wmb, memT], w=[rr])
                k.op("act", lambda e, mt=mt, rr=rr: e.copy(mva[l][:, mt, :, 0:HD], rr.ap.rearrange("p (h d) -> p h d", h=MH)), r=[rr], w=[mva[l]])

        if do_l1:
            w1st = [k.sb("w1st%d" % i, [128, 1024], BF16) for i in range(2)]
            w1i = [0]

            def w1dst(kc, c0, n):
                t_ = w1st[w1i[0] % 2]
                return t_[:, 0:n]

            def w1post(kc, c0, n):
                t_ = w1st[w1i[0] % 2]
                w1i[0] += 1
                k.dma(w1b_d[:, kc, c0:c0 + n], t_[:, 0:n], r=[t_])
            for kc_ in range(1):
                pass
            prep_weight(win1_d, MOBA_IN, 16, w1dst, [w1st[0], w1st[1]], post=w1post)
        k.barrier()
        es_s.close()
        k.es = es
        small.clear()

        def mem_attention(l, mqT, mq_bufs, ntok_tiles, m_tok):
            pm = [rbank.next() for _ in range(ntok_tiles)]
            for hm in range(MH):
                pb = (hm % 2) * 64
                for mt in range(2):
                    rs = r256.next()
                    k.op("pe", lambda e, rs=rs, mt=mt, hm=hm, pb=pb: e.matmul(rs.ap, mkT[l][pb:pb + 64, hm // 2, mt * 128:(mt + 1) * 128],
                                                                          mqT[pb:pb + 64, hm // 2, :], start=True, stop=True),
                         r=[mkT[l]] + mq_bufs, w=[rs])
                    pt = smalltile("ma_pt", [128, 256], BF16, n=(2 if l == 0 else 1))
                    k.op("act", lambda e, rs=rs, pt=pt: e.activation(out=pt[:], in_=rs.ap, func=AF.Exp, scale=0.125), r=[rs], w=[pt])
                    for t in range(ntok_tiles):
                        k.op("pe", lambda e, t=t, pt=pt, mt=mt, hm=hm: e.matmul(pm[t].ap[:, hm * 65:(hm + 1) * 65], pt[:, t * 128:(t + 1) * 128],
                                                                             mva[l][:, mt, hm, :], start=(mt == 0), stop=(mt == 1)),
                             r=[pt, mva[l]], w=[pm[t]])
                    yield
            for t in range(ntok_tiles):
                rden = smalltile("ma_rden", [128, MH])
                pv = pm[t].ap[:, 0:MH * 65].rearrange("p (h d) -> p h d", h=MH)
                k.op("dve", lambda e, pv=pv, rden=rden: e.reciprocal(rden[:], pv[:, :, 64]), r=[pm[t]], w=[rden])
                k.op("dve", lambda e, pv=pv, rden=rden, t=t: e.tensor_tensor(out=m_tok[:, t, :, :], in0=pv[:, :, 0:64],
                                                                           in1=rden[:].unsqueeze(2).to_broadcast([128, MH, HD]), op=ALU.mult),
                     r=[pm[t], rden], w=[m_tok])
            yield

        def out_proj(y, y_bufs, woutb, res_ap, res_bufs, dst, dst_bufs):
            def tr(e):
                ins = None
                for kc in range(8):
                    ins = e.transpose(PTB[:, kc * 128:(kc + 1) * 128], y[:, kc * 128:(kc + 1) * 128], identb[:])
                return ins
            k.op("pe", tr, r=list(y_bufs) + [identb], w=[PTB])
            yT = smalltile("yT", [128, 8, 128], BF16, n=1)
            k.op("act", lambda e: e.copy(yT[:].rearrange("p k t -> p (k t)"), PTB[:]), r=[PTB], w=[yT])
            for nb in range(2):
                rr = rbank.next()

                def mm(e, nb=nb, rr=rr):
                    ins = None
                    for kc in range(8):
                        ins = e.matmul(rr.ap, yT[:, kc, :], woutb[:, kc, nb * 512:(nb + 1) * 512], start=(kc == 0), stop=(kc == 7))
                    return ins
                k.op("pe", mm, r=[yT, woutb], w=[rr])
                k.op("dve", lambda e, nb=nb, rr=rr: e.tensor_tensor(out=dst[:, nb * 512:(nb + 1) * 512], in0=rr.ap, in1=res_ap[:, nb * 512:(nb + 1) * 512], op=ALU.add),
                     r=[rr] + list(res_bufs), w=dst_bufs)

        if do_l0:
            es0 = ExitStack()
            k.es = es0
            w0b = k.sb("w0b", [128, 8, DELTA_IN], BF16)
            wout0b = k.sb("wout0b", [128, 8, D], BF16)
            es0s = ExitStack()
            k.es = es0s
            alloc_wstage()
            prep_weight(win0_d, DELTA_IN, 0, lambda kc, c0, n: w0b[:, kc, c0:c0 + n], [w0b])
            prep_weight(wout0_d, D, None, lambda kc, c0, n: wout0b[:, kc, c0:c0 + n], [wout0b])
            k.barrier()
            es0s.close()
            k.es = es0

            xtok = [k.sb("xtok%d" % i, [128, 2, D], F32) for i in range(1)]
            xT = [k.sb("xT%d" % i, [128, 8, 256], BF16) for i in range(1)]
            halo = k.sb("halo", [128, 18, 3], F32)
            halob = [Buf() for _ in range(18)]
            k.op("pool", lambda e: e.memset(halo[:], 0.0), w=halob)
            qkT = k.sb("qkT", [128, 12, 256], F32)
            ktok = k.sb("ktok", [128, 2, 768], F32)
            vtok = k.sb("vtok", [128, 2, 768], F32)
            zs = k.sb("zs", [128, 2, D], BF16)
            mqT = k.sb("mqT", [128, 2, 256], BF16)
            Sst = k.sb("Sst", [128, 6, 128], F32)
            k.op("pool", lambda e: e.memset(Sst[:], 0.0), w=[Sst])
            NIFP = 2
            QS = [{"raw": k.sb("raw%d" % i, [128, 259], F32), "acc": k.sb("cacc%d" % i, [128, 256], F32), "sl": k.sb("sl%d" % i, [128, 256], F32)} for i in range(NIFP)]
            Q2S = [{"sq": k.sb("sq%d" % i, [128, 256], F32), "rn": k.sb("rn%d" % i, [128, 256], F32)} for i in range(1)]
            qkb = [Buf() for _ in range(12)]
            ktb = [Buf() for _ in range(6)]
            vtb = [Buf() for _ in range(6)]
            NIF = 4
            PS = []
            for s_ in range(NIF):
                P = {"h": []}
                for hl in range(2):
                    H = {"XR": [k.sb("XR%d_%d_0" % (s_, hl), [128, 256], F32R), k.sb("XR%d_%d_1" % (s_, hl), [128, 256], F32R)],
                         "YY": k.sb("YY%d_%d" % (s_, hl), [128, 384], F32R),
                         "AT": k.sb("AT%d_%d" % (s_, hl), [128, 128], F32)}
                    for xr_ in H["XR"]:
                        xr_.bx = Buf()
                        xr_.br = Buf()
                    k.op("dve", lambda e, H=H: e.tensor_copy(H["XR"][0][:], onesf[:, 0:256]), r=[onesf], w=[H["XR"][0].bx, H["XR"][0].br])
                    k.op("dve", lambda e, H=H: e.tensor_copy(H["XR"][1][:], onesf[:, 0:256]), r=[onesf], w=[H["XR"][1].bx, H["XR"][1].br])
                    k.op("dve", lambda e, H=H: e.tensor_copy(H["YY"][:, 0:256], onesf[:, 0:256]), r=[onesf], w=[H["YY"]])
                    k.op("dve", lambda e, H=H: e.tensor_copy(H["YY"][:, 256:384], onesf[:, 0:128]), r=[onesf], w=[H["YY"]])
                    P["h"].append(H)
                for nm in ("vb", "kbg", "kdec", "nwT", "vnew", "o2", "opair", "osq"):
                    P[nm] = k.sb("%s%d" % (nm, s_), [128, 128], F32)
                P["oss"] = k.sb("oss%d" % s_, [128, 2], F32)
                PS.append(P)
            ybfb = [Buf() for _ in range(6)]
            mtok = k.sb("mtok", [128, 2, MH, HD], F32)
            ybf = k.sb("ybf", [128, D], BF16)
            junk["ap"], junk["bufs"] = ybf[:], [ybf] + ybfb

            for c in range(NCH):
                xt = xtok[0]
                xTc = xT[0]
                for t in range(2):
                    tt = 2 * c + t
                    k.dma(xt[:, t, :], x_d[tt * 128:(tt + 1) * 128, :], w=[xt])
                for t in range(2):
                    norm_transpose(xt[:, t, :], [xt], xTc, t * 128)

                def ft_factory(ft):
                    def fac(slot):
                        return ft_task(ft, QS[slot])
                    return fac

                def ft_task(ft, Q):
                    raw, acc = Q["raw"], Q["acc"]
                    rr = rproj.next()

                    def mm(e):
                        ins = None
                        for kc in range(8):
                            ins = e.matmul(rr.ap, w0b[:, kc, ft * 128:(ft + 1) * 128], xTc[:, kc, :], start=(kc == 0), stop=(kc == 7))
                        return ins
                    k.op("pe", mm, r=[w0b, xTc], w=[rr])
                    k.op("pool", lambda e: e.tensor_copy(raw[:, 0:3], halo[:, ft, :]), r=[halob[ft]], w=[raw])
                    k.op("act", lambda e: e.copy(raw[:, 3:259], rr.ap), r=[rr], w=[raw])
                    k.op("pool", lambda e: e.tensor_copy(halo[:, ft, :], raw[:, 256:259]), r=[raw], w=[halob[ft]])
                    yield
                    k.op("dve", lambda e: e.tensor_scalar(out=acc[:], in0=raw[:, 0:256], scalar1=cw[:, ft:ft + 1], scalar2=None, op0=ALU.mult), r=[raw, cw], w=[acc])
                    for tap in range(1, 4):
                        k.op("dve", lambda e, tap=tap: e.scalar_tensor_tensor(out=acc[:], in0=raw[:, tap:tap + 256], scalar=cw[:, tap * 18 + ft:tap * 18 + ft + 1], in1=acc[:],
                                                                         op0=ALU.mult, op1=ALU.add), r=[raw, cw, acc], w=[acc])
                    yield
                    if ft < 12:
                        k.op("act", lambda e: e.activation(out=qkT[:, ft, :].bitcast(F32R), in_=acc[:], func=AF.Silu), r=[acc], w=[qkb[ft]])
                    else:
                        sl = Q["sl"]
                        fo = ft - 12
                        k.op("act", lambda e: e.activation(out=sl[:], in_=acc[:], func=AF.Silu), r=[acc], w=[sl])
                        yield
                        for t in range(2):
                            rt = r128.next()
                            k.op("pe", lambda e, t=t, rt=rt: e.transpose(rt.ap, sl[:, t * 128:(t + 1) * 128], identf[:]), r=[sl, identf], w=[rt])
                            if t == 0:
                                k.op("act", lambda e, t=t, rt=rt: e.copy(vtok[:, t, fo * 128:(fo + 1) * 128], rt.ap), r=[rt], w=[vtb[fo]])
                            else:
                                k.op("dve", lambda e, t=t, rt=rt: e.tensor_copy(vtok[:, t, fo * 128:(fo + 1) * 128], rt.ap), r=[rt], w=[vtb[fo]])
                    yield

                run_tasks([ft_factory(ft) for ft in range(18)], NIFP, stagger=1)
                for ft in range(2):
                    rr = rproj.next()
                    c0 = 2304 + 1024 + ft * 128

                    def mm(e, c0=c0, rr=rr):
                        ins = None
                        for kc in range(8):
                            ins = e.matmul(rr.ap, w0b[:, kc, c0:c0 + 128], xTc[:, kc, :], start=(kc == 0), stop=(kc == 7))
                        return ins
                    k.op("pe", mm, r=[w0b, xTc], w=[rr])
                    k.op("act", lambda e, ft=ft, rr=rr: e.copy(mqT[:, ft, :], rr.ap), r=[rr], w=[mqT])
                for t in range(2):
                    for nb in range(2):
                        rr = rbank.next()
                        c0 = 2304 + nb * 512

                        def mm(e, c0=c0, rr=rr, t=t):
                            ins = None
                            for kc in range(8):
                                ins = e.matmul(rr.ap, xTc[:, kc, t * 128:(t + 1) * 128], w0b[:, kc, c0:c0 + 512], start=(kc == 0), stop=(kc == 7))
                            return ins
                        k.op("pe", mm, r=[w0b, xTc], w=[rr])
                        k.op("act", lambda e, t=t, nb=nb, rr=rr: e.activation(out=zs[:, t, nb * 512:(nb + 1) * 512], in_=rr.ap, func=AF.Silu), r=[rr], w=[zs])

                def l2_factory(ft):
                    def fac(slot):
                        return l2_task(ft, Q2S[slot])
                    return fac

                def l2_task(ft, Q2):
                    sq, rn = Q2["sq"], Q2["rn"]
                    k.op("act", lambda e: e.activation(out=sq[:], in_=qkT[:, ft, :], func=AF.Square), r=[qkb[ft]], w=[sq])
                    yield
                    rs = r256.next()
                    k.op("pe", lambda e: e.matmul(rs.ap, blockones[:], sq[:], start=True, stop=True), r=[blockones, sq], w=[rs])
                    k.op("act", lambda e: e.activation(out=rn[:], in_=rs.ap, func=AF.Ln, bias=EPS), r=[rs], w=[rn])
                    yield
                    k.op("act", lambda e: e.activation(out=rn[:], in_=rn[:], func=AF.Exp, scale=-0.5), r=[rn], w=[rn])
                    yield
                    sc = 0.125 if ft < 6 else 1.0
                    k.op("dve", lambda e: e.scalar_tensor_tensor(out=qkT[:, ft, :].bitcast(F32R), in0=qkT[:, ft, :], scalar=sc, in1=rn[:], op0=ALU.mult, op1=ALU.mult),
                         r=[qkb[ft], rn], w=[qkb[ft]])
                    if ft >= 6:
                        yield
                        fo = ft - 6
                        for t in range(2):
                            rt = r128.next()
                            k.op("pe", lambda e, t=t, rt=rt: e.transpose(rt.ap, qkT[:, ft, t * 128:(t + 1) * 128], identf[:]), r=[qkb[ft], identf], w=[rt])
                            if t == 0:
                                k.op("act", lambda e, t=t, rt=rt: e.copy(ktok[:, t, fo * 128:(fo + 1) * 128], rt.ap), r=[rt], w=[ktb[fo]])
                            else:
                                k.op("dve", lambda e, t=t, rt=rt: e.tensor_copy(ktok[:, t, fo * 128:(fo + 1) * 128], rt.ap), r=[rt], w=[ktb[fo]])
                    yield


                Gt = [smalltile("gates", [128, 12, NH], F32, n=2) for _ in range(2)]

                def gate_task(t):
                    G = Gt[t]
                    gb_ = [G]
                    rgate = r128.next()

                    def mmg(e):
                        ins = None
                        for kc in range(8):
                            ins = e.matmul(rgate.ap[:, 0:24], xTc[:, kc, t * 128:(t + 1) * 128], w0b[:, kc, 3584:3608], start=(kc == 0), stop=(kc == 7))
                        return ins
                    k.op("pe", mmg, r=[w0b, xTc], w=[rgate])
                    k.op("dve", lambda e: e.tensor_tensor(out=G[:, 0, :], in0=rgate.ap[:, 12:24], in1=dtb_bc[:], op=ALU.add), r=[rgate, dtb_bc], w=gb_)
                    k.op("act", lambda e: e.activation(out=G[:, 1, :], in_=rgate.ap[:, 0:12], func=AF.Exp, scale=-1.0), r=[rgate], w=gb_)
                    yield
                    k.op("act", lambda e: e.activation(out=G[:, 0, :], in_=G[:, 0, :], func=AF.Exp), r=gb_, w=gb_)
                    k.op("dve", lambda e: e.tensor_scalar(out=G[:, 1, :], in0=G[:, 1, :], scalar1=1.0, scalar2=None, op0=ALU.add), r=gb_, w=gb_)
                    yield
                    k.op("act", lambda e: e.activation(out=G[:, 0, :], in_=G[:, 0, :], func=AF.Ln, bias=1.0), r=gb_, w=gb_)
                    k.op("dve", lambda e: e.reciprocal(G[:, 2, :], G[:, 1, :]), r=gb_, w=gb_)
                    yield
                    k.op("dve", lambda e: e.tensor_tensor(out=G[:, 0, :], in0=G[:, 0, :], in1=nA_bc[:], op=ALU.mult), r=gb_ + [nA_bc], w=gb_)
                    k.op("act", lambda e: e.activation(out=G[:, 3, :], in_=G[:, 1, :], func=AF.Ln), r=gb_, w=gb_)
                    yield
                    rgc = r128.next()
                    k.op("pe", lambda e: e.matmul(rgc.ap[:, 0:NH], tri[:], G[:, 0, :], start=True, stop=True), r=[tri] + gb_, w=[rgc])
                    k.op("pe", lambda e: e.matmul(rgc.ap[:, 16:16 + NH], onesf[:, 0:128], G[:, 0, :], start=True, stop=True), r=[onesf] + gb_, w=[rgc])
                    k.op("dve", lambda e: e.tensor_copy(G[:, 4, :], rgc.ap[:, 0:NH]), r=[rgc], w=gb_)
                    k.op("act", lambda e: e.activation(out=G[:, 5, :], in_=rgc.ap[:, 0:NH], func=AF.Exp), r=[rgc], w=gb_)
                    k.op("act", lambda e: e.activation(out=G[:, 8, :], in_=rgc.ap[:, 16:16 + NH], func=AF.Exp), r=[rgc], w=gb_)
                    k.op("dve", lambda e: e.tensor_tensor(out=G[:, 7, :], in0=rgc.ap[:, 16:16 + NH], in1=G[:, 4, :], op=ALU.subtract), r=[rgc] + gb_, w=gb_)
                    yield
                    k.op("dve", lambda e: e.tensor_tensor(out=G[:, 6, :], in0=G[:, 2, :], in1=G[:, 5, :], op=ALU.mult), r=gb_, w=gb_)
                    k.op("act", lambda e: e.activation(out=G[:, 7, :], in_=G[:, 7, :], func=AF.Exp), r=gb_, w=gb_)
                    k.op("dve", lambda e: e.tensor_tensor(out=G[:, 9, :], in0=G[:, 4, :], in1=G[:, 3, :], op=ALU.subtract), r=gb_, w=gb_)
                    k.op("dve", lambda e: e.tensor_scalar(out=G[:, 10, :], in0=G[:, 4, :], scalar1=-1.0, scalar2=None, op0=ALU.mult), r=gb_, w=gb_)
                    yield

                run_tasks([lambda slot: gate_task(0), lambda slot: gate_task(1), lambda slot: mem_attention(0, mqT, [mqT], 2, mtok)]
                          + [("l2", l2_factory(ft)) for ft in range(12)], 5, stagger=0, pools={"l2": 1})

                for t in range(2):
                    ts = slice(t * 128, (t + 1) * 128)
                    G = Gt[t]
                    gb_ = [G]
                    tt = 2 * c + t
                    yT = smalltile("yT", [128, 8, 128], BF16, n=1)
                    k.op("dve", lambda e: e.tensor_tensor(out=ybf[:, 768:1024], in0=mtok[:, t, :, :].rearrange("p h d -> p (h d)"), in1=zs[:, t, 768:1024], op=ALU.mult),
                         r=[mtok, zs], w=[ybf])

                    def trm(e):
                        e.transpose(PTB[:, 768:896], ybf[:, 768:896], identb[:])
                        return e.transpose(PTB[:, 896:1024], ybf[:, 896:1024], identb[:])
                    k.op("pe", trm, r=[ybf, identb], w=[PTB])
                    k.op("act", lambda e: e.copy(yT[:, 6:8, :], PTB[:, 768:1024].rearrange("p (k t) -> p k t", k=2)), r=[PTB], w=[yT])
                    pacc = [rbank.next(), rbank.next()]
                    opn = [0]

                    def oproj(kcs, last):
                        for nb in range(2):
                            def mm(e, nb=nb):
                                ins = None
                                for i_, kc in enumerate(kcs):
                                    first = (opn[0] == 0 and i_ == 0)
                                    ins = e.matmul(pacc[nb].ap, yT[:, kc, :], wout0b[:, kc, nb * 512:(nb + 1) * 512], start=first, stop=(last and i_ == len(kcs) - 1),
                                                   skip_group_check=True)
                                return ins
                            k.op("pe", mm, r=[yT, wout0b], w=[pacc[nb]])
                        opn[0] += 1
                    oproj([6, 7], False)
                    npair_done = [0]

                    def pair_task(hp, P):
                        cs = slice(hp * 128, (hp + 1) * 128)
                        k3p = ktok[:, t, cs].rearrange("p (h d) -> p h d", h=2)
                        v3p = vtok[:, t, cs].rearrange("p (h d) -> p h d", h=2)

                        def bcp(slot):
                            return G[:, slot, 2 * hp:2 * hp + 2].unsqueeze(2).to_broadcast([128, 2, HD])
                        vb, kbg, kdec = P["vb"], P["kbg"], P["kdec"]
                        k.op("dve", lambda e: e.tensor_tensor(out=vb[:].rearrange("p (h d) -> p h d", h=2), in0=v3p, in1=bcp(2), op=ALU.mult), r=[vtb[hp]] + gb_, w=[vb])
                        k.op("dve", lambda e: e.tensor_tensor(out=kbg[:].rearrange("p (h d) -> p h d", h=2), in0=k3p, in1=bcp(6), op=ALU.mult), r=[ktb[hp]] + gb_, w=[kbg])
                        for hl_ in range(2):
                            k.op("act", lambda e, hl_=hl_: e.activation(out=kdec[:, hl_ * 64:(hl_ + 1) * 64], in_=ktok[:, t, hp * 128 + hl_ * 64:hp * 128 + (hl_ + 1) * 64], func=AF.Copy,
                                                                     scale=G[:, 7, 2 * hp + hl_:2 * hp + hl_ + 1]), r=[ktb[hp]] + gb_, w=[kdec])
                        for hl in range(2):
                            h = 2 * hp + hl
                            pb = hl * 64
                            H = P["h"][hl]
                            GG = smalltile("GG", [128, 256], F32, n=2)
                            k.op("act", lambda e: e.activation(out=GG[:, 0:128], in_=identf[:], func=AF.Copy, scale=G[:, 4, h:h + 1]), r=[identf] + gb_, w=[GG])
                            k.op("act", lambda e: e.activation(out=GG[:, 128:256], in_=identf[:], func=AF.Copy, scale=G[:, 9, h:h + 1]), r=[identf] + gb_, w=[GG])
                            rD = r256.next()

                            def mmD(e):
                                e.matmul(rD.ap, onesf[:, 0:128], GG[:], start=True, stop=False)
                                return e.matmul(rD.ap, identr[:], negmaskr[:], start=False, stop=True)
                            k.op("pe", mmD, r=[onesf, GG, identr, negmaskr], w=[rD])
                            EE = smalltile("EE", [128, 256], F32, n=2)
                            k.op("act", lambda e: e.activation(out=EE[:], in_=rD.ap, func=AF.Exp, bias=G[:, 10, h:h + 1]), r=[rD] + gb_, w=[EE])
                            rK = r256.next()

                            def mmK(e):
                                return e.matmul(rK.ap.rearrange("p (a b) -> p a b", a=2), qkT[pb:pb + 64, 6 + hp, ts].bitcast(F32R),
                                                qkT[pb:pb + 64, hp:12:6, ts].bitcast(F32R), start=True, stop=True)
                            k.op("pe", mmK, r=[qkb[hp], qkb[6 + hp]], w=[rK])
                            XR0, XR1, YY = H["XR"][0], H["XR"][1], H["YY"]
                            k.op("dve", lambda e: e.tensor_tensor(out=H["AT"][:], in0=rK.ap[:, 0:128], in1=EE[:, 0:128], op=ALU.mult), r=[rK, EE], w=[H["AT"]])
                            k.op("dve", lambda e: e.tensor_tensor(out=XR0[:, 0:128], in0=rK.ap[:, 128:256], in1=EE[:, 128:256], op=ALU.mult), r=[rK, EE], w=[XR0.bx])
                            rY = r128.next()
                            k.op("pe", lambda e: e.transpose(rY.ap.bitcast(F32R), XR0[:, 0:128], identr[:]), r=[XR0.bx, identr], w=[rY])
                            k.op("act", lambda e: e.copy(YY[:, 0:128], rY.ap), r=[rY], w=[YY])
                            k.op("dve", lambda e: e.tensor_tensor(out=XR1[:, 128:256], in0=identf[:], in1=XR0[:, 0:128], op=ALU.subtract), r=[identf, XR0.bx], w=[XR1.br])
                        yield
                        for hl in range(2):
                            H = P["h"][hl]
                            XR0, XR1, YY = H["XR"][0], H["XR"][1], H["YY"]
                            rA = r256.next()
                            k.op("pe", lambda e: e.matmul(rA.ap, YY[:, 0:128], XR0[:, :], start=True, stop=True), r=[YY, XR0.bx, XR0.br], w=[rA])
                            k.op("act", lambda e: e.copy(XR1[:, 0:128], rA.ap[:, 0:128]), r=[rA], w=[XR1.bx])
                            rB = r256.next()
                            k.op("pe", lambda e: e.matmul(rB.ap, XR0[:, 0:128], YY[:, 0:256], start=True, stop=True), r=[YY, XR0.bx], w=[rB])
                            k.op("dve", lambda e: e.tensor_copy(YY[:, 128:256], rB.ap[:, 0:128]), r=[rB], w=[YY])
                        yield
                        for lev in range(1, 7):
                            for hl in range(2):
                                H = P["h"][hl]
                                XRj, XRn, YY = H["XR"][lev % 2], H["XR"][(lev + 1) % 2], H["YY"]
                                ys = (lev % 2) * 128
                                yn = ((lev + 1) % 2) * 128
                                rA = r256.next()
                                k.op("pe", lambda e: e.matmul(rA.ap, YY[:, ys:ys + 128], XRj[:, :], start=True, stop=True), r=[YY, XRj.bx, XRj.br], w=[rA])
                                if lev < 6:
                                    k.op("act", lambda e: e.copy(XRn[:, 0:128], rA.ap[:, 0:128]), r=[rA], w=[XRn.bx])
                                    k.op("dve", lambda e: e.tensor_tensor(out=XRn[:, 128:256], in0=rA.ap[:, 128:256], in1=XRj[:, 128:256], op=ALU.add), r=[rA, XRj.br], w=[XRn.br])
                                    rB = r256.next()
                                    k.op("pe", lambda e: e.matmul(rB.ap, XRj[:, 0:128], YY[:, ys:ys + 256], start=True, stop=True), r=[YY, XRj.bx], w=[rB])
                                    k.op("dve", lambda e: e.tensor_copy(YY[:, yn:yn + 128], rB.ap[:, 0:128]), r=[rB], w=[YY])
                                else:
                                    k.op("dve", lambda e: e.tensor_tensor(out=XRn[:, 128:256], in0=rA.ap[:, 128:256], in1=XRj[:, 128:256], op=ALU.add), r=[rA, XRj.br], w=[XRn.br])
                            yield
                        Rs = [View(P["h"][0]["XR"][1][:, 128:256].bitcast(F32)), View(P["h"][1]["XR"][1][:, 128:256].bitcast(F32))]
                        Rs[0].b = P["h"][0]["XR"][1].br
                        Rs[1].b = P["h"][1]["XR"][1].br
                        ATs = [P["h"][0]["AT"], P["h"][1]["AT"]]
                        nwT, vnew, o2, opair = P["nwT"], P["vnew"], P["o2"], P["opair"]
                        for hl in range(2):
                            rW = r128.next()
                            k.op("pe", lambda e: e.matmul(rW.ap, kbg[:], Rs[hl][:], start=True, stop=True), r=[kbg, Rs[hl]], w=[rW])
                            pb = hl * 64
                            k.op("act", lambda e: e.activation(out=nwT[pb:pb + 64, :], in_=rW.ap[pb:pb + 64, :], func=AF.Copy, scale=-1.0), r=[rW], w=[nwT])
                        yield
                        rV = r128.next()

                        def mmV(e):
                            for hl in range(2):
                                e.matmul(rV.ap[:, hl * 64:(hl + 1) * 64], Rs[hl][:], vb[:, hl * 64:(hl + 1) * 64], start=(hl == 0), stop=False)
                            return e.matmul(rV.ap, nwT[:], Sst[:, hp, :], start=False, stop=True)
                        k.op("pe", mmV, r=[Rs[0], Rs[1], vb, nwT, Sst], w=[rV])
                        k.op("act", lambda e: e.copy(vnew[:], rV.ap), r=[rV], w=[vnew])
                        yield
                        rO1 = r128.next()
                        k.op("pe", lambda e: e.matmul(rO1.ap, qkT[:, hp, ts], Sst[:, hp, :], start=True, stop=True), r=[qkb[hp], Sst], w=[rO1])
                        rO2 = r128.next()

                        def mmO2(e):
                            ins = None
                            for hl in range(2):
                                ins = e.matmul(rO2.ap[:, hl * 64:(hl + 1) * 64], ATs[hl][:], vnew[:, hl * 64:(hl + 1) * 64], start=True, stop=True)
                            return ins
                        k.op("pe", mmO2, r=[ATs[0], ATs[1], vnew], w=[rO2])
                        k.op("act", lambda e: e.copy(o2[:], rO2.ap), r=[rO2], w=[o2])
                        for hl in range(2):
                            h = 2 * hp + hl
                            k.op("dve", lambda e: e.scalar_tensor_tensor(out=opair[:, hl * 64:(hl + 1) * 64], in0=rO1.ap[:, hl * 64:(hl + 1) * 64], scalar=G[:, 5, h:h + 1],
                                                                        in1=o2[:, hl * 64:(hl + 1) * 64], op0=ALU.mult, op1=ALU.add),
                                 r=[rO1, o2] + gb_, w=[opair])
                        yield
                        rS = r128.next()
                        k.op("pe", lambda e: e.matmul(rS.ap, kdec[:], vnew[:], start=True, stop=True), r=[kdec, vnew], w=[rS])
                        for hl in range(2):
                            h = 2 * hp + hl
                            pb = hl * 64
                            k.op("dve", lambda e: e.scalar_tensor_tensor(out=Sst[pb:pb + 64, hp, pb:pb + 64], in0=Sst[pb:pb + 64, hp, pb:pb + 64],
                                                                        scalar=G[pb:pb + 64, 8, h:h + 1], in1=rS.ap[pb:pb + 64, pb:pb + 64], op0=ALU.mult, op1=ALU.add),
                                 r=[rS, Sst] + gb_, w=[Sst])
                        osq, oss = P["osq"], P["oss"]
                        k.op("act", lambda e: e.activation(out=osq[:], in_=opair[:], func=AF.Square), r=[opair], w=[osq])
                        k.op("dve", lambda e: e.tensor_reduce(out=oss[:], in_=osq[:].rearrange("p (h d) -> p h d", h=2), axis=mybir.AxisListType.X, op=ALU.add), r=[osq], w=[oss])
                        yield
                        k.op("act", lambda e: e.activation(out=oss[:], in_=oss[:], func=AF.Ln, scale=1.0 / HD, bias=EPS), r=[oss], w=[oss])
                        k.op("act", lambda e: e.activation(out=oss[:], in_=oss[:], func=AF.Exp, scale=-0.5), r=[oss], w=[oss])
                        for hl in range(2):
                            k.op("dve", lambda e: e.scalar_tensor_tensor(out=osq[:, hl * 64:(hl + 1) * 64], in0=opair[:, hl * 64:(hl + 1) * 64], scalar=oss[:, hl:hl + 1],
                                                                        in1=onorm_bc[:], op0=ALU.mult, op1=ALU.mult), r=[opair, oss, onorm_bc, osq], w=[osq])
                        k.op("dve", lambda e: e.tensor_tensor(out=ybf[:, cs], in0=osq[:], in1=zs[:, t, cs], op=ALU.mult), r=[osq, zs], w=[ybfb[hp]])
                        yield
                        k.op("pe", lambda e: e.transpose(PTB[:, cs], ybf[:, cs], identb[:]), r=[ybfb[hp], identb], w=[PTB])
                        k.op("act", lambda e: e.copy(yT[:, hp, :], PTB[:, cs]), r=[PTB], w=[yT])
                        npair_done[0] += 1
                        oproj([hp], npair_done[0] == 6)
                        yield

                    def pair_factory(hp):
                        def fac(slot):
                            return pair_task(hp, PS[slot])
                        return fac
                    run_tasks([pair_factory(hp) for hp in range(6)], NIF, stagger=2)
                    for nb in range(2):
                        k.op("dve", lambda e, nb=nb: e.tensor_tensor(out=xt[:, t, nb * 512:(nb + 1) * 512], in0=pacc[nb].ap, in1=xt[:, t, nb * 512:(nb + 1) * 512], op=ALU.add),
                             r=[pacc[nb], xt], w=[xt])
                    k.dma(h1_d[tt * 128:(tt + 1) * 128, :], xt[:, t, :], r=[xt], is_output=(not do_l1))
            k.barrier()
            es0.close()
            k.es = es
            small.clear()

        if do_l1:
            import math
            h1src = h1_d if do_l0 else x_d
            es1 = ExitStack()
            k.es = es1
            wout1b = k.sb("wout1b", [128, 8, D], BF16)
            posf = k.sb("posf", [128, NT], F32)
            es1s = ExitStack()
            k.es = es1s
            alloc_wstage()
            prep_weight(wout1_d, D, None, lambda kc, c0, n: wout1b[:, kc, c0:c0 + n], [wout1b])
            pst_i = k.sb("pst_i", [NT, 128], I32)
            pst_f = k.sb("pst_f", [NT, 128], F32)
            k.dma(pst_i[:], pos_d.rearrange("(t p) -> t p", p=128), w=[pst_i])
            k.op("dve", lambda e: e.tensor_copy(pst_f[:], pst_i[:]), r=[pst_i], w=[pst_f])
            rp = r128.next()
            k.op("pe", lambda e: e.transpose(rp.ap[:, 0:NT], pst_f[0:NT, :], identf[0:NT, 0:NT]), r=[pst_f, identf], w=[rp])
            k.op("dve", lambda e: e.tensor_copy(posf[:], rp.ap[:, 0:NT]), r=[rp], w=[posf])
            k.barrier()
            es1s.close()
            k.es = es1
            small.clear()

            fnorm_bc = k.sb("fnorm_bc", [128, D], F32)
            k.dma(fnorm_bc[:], fnorm_d.partition_broadcast(128), w=[fnorm_bc])
            invf = k.sb("invf", [128, 32], F32)
            for i in range(32):
                k.op("pool", lambda e, i=i: e.memset(invf[:, i:i + 1], float(np.float32(10000.0) ** np.float32(-i / 32.0))), w=[invf])
            kT = k.sb("kT", [128, 6, S], BF16)
            Va = k.sb("Va", [128, NT, NH, HD + 1], BF16)
            k.op("pool", lambda e: e.memset(Va[:], 1.0), w=[Va])
            kmT = k.sb("kmT", [128, 6, 16], BF16)
            k.op("pool", lambda e: e.memset(kmT[:], 0.0), w=[kmT])
            htok = k.sb("htok", [128, 2, D], F32)
            hT = k.sb("hT", [128, 8, 256], BF16)
            wblk = [k.sb("wblk%d" % i, [128, 8, 256], BF16) for i in range(2)]
            qktok = k.sb("qktok", [128, 2, 1536], F32)
            qkrot = k.sb("qkrot", [128, 24, HD], BF16)
            qT = k.sb("qT", [128, 6, 256], BF16)
            zs1 = k.sb("zs1", [128, 2, D], BF16)
            mqT1 = k.sb("mqT1", [128, 2, 256], BF16)
            cs = k.sb("cs", [128, 2, 2, 32], F32)
            sel = k.sb("sel", [128, 2, NH, 16], F32)
            acc = k.sb("acc", [128, 2, NH, HD + 1], F32)
            mtok1 = k.sb("mtok1", [128, 2, MH, HD], F32)
            ybf1 = k.sb("ybf1", [128, D], BF16)
            junk["ap"], junk["bufs"] = ybf1[:], [ybf1]
            rST = Ring([Reg(PB, PB[:, :]), Reg(PD[3], PD[3][:, :]), Reg(PA[0], PA[0][:, :]), Reg(PA[1], PA[1][:, :])])
            rPV = Ring([Reg(PD[0], PD[0][:, 0:130]), Reg(PD[1], PD[1][:, 0:130]), Reg(PD[2], PD[2][:, 0:130])])
            accb = [Buf() for _ in range(NH)]
            NPT = 5
            ptv = [View(qktok[:, 0, j * 256:(j + 1) * 256].bitcast(BF16)) for j in range(NPT)]
            pti = [0]
            Eoh = k.sb("Eoh", [128, 16, 128], BF16)
            k.op("pool", lambda e: e.memset(Eoh[:], 1.0), w=[Eoh])
            for half in range(2):
                k.op("pool", lambda e, half=half: e.affine_select(out=Eoh[half * 64:(half + 1) * 64], in_=Eoh[half * 64:(half + 1) * 64], pattern=[[-1, 16], [0, 128]],
                                                             compare_op=ALU.is_equal, fill=0.0, base=0, channel_multiplier=1), r=[Eoh], w=[Eoh])
            selst = k.sb("selst", [128, 6, 128], BF16)
            k.op("pool", lambda e: e.memset(selst[:], 0.0), w=[selst])
            selT = k.sb("selT", [128, 6, 256], BF16)
            negmaskb = k.sb("negmaskb", [128, 128], BF16)
            k.op("dve", lambda e: e.tensor_copy(negmaskb[:], negmask[:, 0:128]), r=[negmask], w=[negmaskb])
            rGe = Reg(PD[2], PD[2][:, 0:96])
            rGo = Reg(PA[0], PA[0][:, 0:96])
            TWO_PI = 2.0 * math.pi
            C1 = 6.28125
            C2 = TWO_PI - C1
            MAGIC = 12582912.0
            wbi = [0]

            for c in range(NCH):
                if l1stop < 1:
                    continue
                for t in range(2):
                    tt = 2 * c + t
                    k.dma(htok[:, t, :], h1src[tt * 128:(tt + 1) * 128, :], w=[htok])
                for t in range(2):
                    norm_transpose(htok[:, t, :], [htok], hT, t * 128)
                a4 = smalltile("a4", [128, 2, 2, 32], F32, n=1)
                v4 = smalltile("v4", [128, 2, 2, 32], F32, n=1)
                w4 = smalltile("w4", [128, 2, 2, 32], F32, n=1)
                for t in range(2):
                    tt = 2 * c + t
                    k.op("dve", lambda e, t=t, tt=tt: e.tensor_scalar(out=a4[:, t, 1, :], in0=invf[:], scalar1=posf[:, tt:tt + 1], scalar2=None, op0=ALU.mult), r=[invf, posf], w=[a4])
                    k.op("dve", lambda e, t=t: e.tensor_scalar(out=a4[:, t, 0, :], in0=a4[:, t, 1, :], scalar1=math.pi / 2, scalar2=None, op0=ALU.add), r=[a4], w=[a4])
                k.op("dve", lambda e: e.tensor_scalar(out=v4[:], in0=a4[:], scalar1=1.0 / TWO_PI, scalar2=MAGIC, op0=ALU.mult, op1=ALU.add), r=[a4], w=[v4])
                k.op("dve", lambda e: e.tensor_scalar(out=w4[:], in0=v4[:], scalar1=MAGIC, scalar2=-C1, op0=ALU.subtract, op1=ALU.mult), r=[v4], w=[w4])
                k.op("dve", lambda e: e.tensor_tensor(out=a4[:], in0=a4[:], in1=w4[:], op=ALU.add), r=[a4, w4], w=[a4])
                k.op("dve", lambda e: e.tensor_scalar(out=w4[:], in0=v4[:], scalar1=MAGIC, scalar2=-C2, op0=ALU.subtract, op1=ALU.mult), r=[v4], w=[w4])
                k.op("dve", lambda e: e.tensor_tensor(out=a4[:], in0=a4[:], in1=w4[:], op=ALU.add), r=[a4, w4], w=[a4])
                k.op("dve", lambda e: e.tensor_scalar(out=a4[:], in0=a4[:], scalar1=math.pi, scalar2=-math.pi, op0=ALU.min, op1=ALU.max), r=[a4], w=[a4])
                k.op("act", lambda e: e.activation(out=cs[:], in_=a4[:], func=AF.Sin), r=[a4], w=[cs])

                if l1stop < 2:
                    continue
                for blk in range(13):
                    wb = wblk[wbi[0] % 2]
                    wbi[0] += 1
                    k.dma(wb[:], w1b_d[:, :, blk * 256:(blk + 1) * 256], w=[wb])
                    for t in range(2):
                        tt = 2 * c + t
                        rr = rproj.next()

                        def mm(e, rr=rr, t=t, wb=wb):
                            ins = None
                            for kc in range(8):
                                ins = e.matmul(rr.ap, hT[:, kc, t * 128:(t + 1) * 128], wb[:, kc, :], start=(kc == 0), stop=(kc == 7))
                            return ins
                        k.op("pe", mm, r=[hT, wb], w=[rr])
                        if blk < 6:
                            k.op("act", lambda e, rr=rr, t=t, blk=blk: e.copy(qktok[:, t, blk * 256:(blk + 1) * 256], rr.ap), r=[rr], w=[qktok] + ptv)
                        elif blk < 9:
                            h0 = (blk - 6) * 4
                            k.op("act", lambda e, rr=rr, tt=tt, h0=h0: e.copy(Va[:, tt, h0:h0 + 4, 0:HD], rr.ap.rearrange("p (h d) -> p h d", h=4)), r=[rr], w=[Va])
                        else:
                            z0 = (blk - 9) * 256
                            k.op("act", lambda e, rr=rr, t=t, z0=z0: e.activation(out=zs1[:, t, z0:z0 + 256], in_=rr.ap, func=AF.Silu), r=[rr], w=[zs1])
                wb = wblk[wbi[0] % 2]
                wbi[0] += 1
                k.dma(wb[:], w1b_d[:, :, 3328:3584], w=[wb])
                for ft in range(2):
                    rr = rproj.next()

                    def mmq(e, rr=rr, ft=ft, wb=wb):
                        ins = None
                        for kc in range(8):
                            ins = e.matmul(rr.ap, wb[:, kc, ft * 128:(ft + 1) * 128], hT[:, kc, :], start=(kc == 0), stop=(kc == 7))
                        return ins
                    k.op("pe", mmq, r=[hT, wb], w=[rr])
                    k.op("act", lambda e, rr=rr, ft=ft: e.copy(mqT1[:, ft, :], rr.ap), r=[rr], w=[mqT1])

                if l1stop < 3:
                    continue
                for t in range(2):
                    tt = 2 * c + t
                    x3 = qktok[:, t, :].rearrange("p (h d) -> p h d", h=24)
                    cosb = cs[:, t, 0, :].unsqueeze(1).to_broadcast([128, 24, 32])
                    sinb = cs[:, t, 1, :].unsqueeze(1).to_broadcast([128, 24, 32])
                    accf = acc[:].rearrange("p q h d -> p (q h d)")
                    t1 = accf[:, 0:768].rearrange("p (h d) -> p h d", h=24)
                    t2 = accf[:, 768:1536].rearrange("p (h d) -> p h d", h=24)
                    k.op("dve", lambda e: e.tensor_tensor(out=t1, in0=x3[:, :, 0:32], in1=cosb, op=ALU.mult), r=[qktok, cs], w=accb)
                    k.op("dve", lambda e: e.tensor_tensor(out=t2, in0=x3[:, :, 32:64], in1=sinb, op=ALU.mult), r=[qktok, cs], w=accb)
                    k.op("dve", lambda e: e.tensor_tensor(out=qkrot[:, :, 0:32], in0=t1, in1=t2, op=ALU.subtract), r=accb, w=[qkrot])
                    k.op("dve", lambda e: e.tensor_tensor(out=t1, in0=x3[:, :, 32:64], in1=cosb, op=ALU.mult), r=[qktok, cs], w=accb)
                    k.op("dve", lambda e: e.tensor_tensor(out=t2, in0=x3[:, :, 0:32], in1=sinb, op=ALU.mult), r=[qktok, cs], w=accb)
                    k.op("dve", lambda e: e.tensor_tensor(out=qkrot[:, :, 32:64], in0=t1, in1=t2, op=ALU.add), r=accb, w=[qkrot])
                    qk2 = qkrot[:].rearrange("p h d -> p (h d)")
                    for grp in range(2):
                        def tr(e, grp=grp):
                            ins = None
                            for j in range(6):
                                cc = (grp * 6 + j) * 128
                                ins = e.transpose(PTB[:, j * 128:(j + 1) * 128], qk2[:, cc:cc + 128], identb[:])
                            return ins
                        k.op("pe", tr, r=[qkrot, identb], w=[PTB])
                        src = PTB[:, 0:768].rearrange("p (j t) -> p j t", j=6)
                        if grp == 0:
                            k.op("act", lambda e, t=t, src=src: e.copy(qT[:, :, t * 128:(t + 1) * 128], src), r=[PTB], w=[qT])
                        else:
                            k.op("dve", lambda e, tt=tt, src=src: e.tensor_copy(kT[:, :, tt * 128:(tt + 1) * 128], src), r=[PTB], w=[kT])
                ksum = smalltile("ksum", [128, 6], F32, n=2)
                k.op("dve", lambda e: e.tensor_reduce(out=ksum[:], in_=kT[:, :, c * 256:(c + 1) * 256], axis=mybir.AxisListType.X, op=ALU.add), r=[kT], w=[ksum])

                if l1stop < 4:
                    continue
                for _ in mem_attention(1, mqT1, [mqT1], 2, mtok1):
                    pass
                if l1stop < 5:
                    continue

                if c >= 1:
                    for t in range(2):
                        gpad = smalltile("gpad", [128, NH, 16], F32, n=1)
                        top8 = smalltile("top8", [128, NH, 8], F32, n=1)
                        k.op("pool", lambda e: e.memset(gpad[:], -1e30), w=[gpad])
                        gp4 = gpad[:].rearrange("p (a b) c -> p a b c", b=2)
                        for hl, rG in ((0, rGe), (1, rGo)):
                            pb = hl * 64

                            def mmg(e, t=t, rG=rG, pb=pb):
                                ins = None
                                for hp_ in range(6):
                                    ins = e.matmul(rG.ap[:, hp_ * 16:(hp_ + 1) * 16], qT[pb:pb + 64, hp_, t * 128:(t + 1) * 128], kmT[pb:pb + 64, hp_, :], start=True, stop=True)
                                return ins
                            k.op("pe", mmg, r=[qT, kmT], w=[rG])
                            k.op("dve", lambda e, rG=rG, hl=hl: e.tensor_copy(gp4[:, :, hl, 0:c], rG.ap.rearrange("p (h b) -> p h b", h=6)[:, :, 0:c]), r=[rG, gpad], w=[gpad])
                        for h in range(NH):
                            k.op("dve", lambda e, h=h: e.max(out=top8[:, h, :], in_=gpad[:, h, :]), r=[gpad], w=[top8])
                        k.op("dve", lambda e, t=t: e.tensor_tensor(out=sel[:, t, :, :], in0=gpad[:], in1=top8[:, :, 2:3].to_broadcast([128, NH, 16]), op=ALU.is_ge), r=[gpad, top8], w=[sel])
                        k.op("dve", lambda e, t=t: e.tensor_scalar(out=selst[:].rearrange("p a (b c) -> p a b c", b=2)[:, :, :, 0:16],
                                                               in0=sel[:, t, :, :].rearrange("p (a b) c -> p a b c", b=2), scalar1=-NEG, scalar2=NEG, op0=ALU.mult, op1=ALU.add),
                             r=[sel], w=[selst])

                        def trs(e):
                            ins = None
                            for j in range(6):
                                ins = e.transpose(PTB[:, j * 128:(j + 1) * 128], selst[:, j, :], identb[:])
                            return ins
                        k.op("pe", trs, r=[selst, identb], w=[PTB])
                        k.op("act", lambda e, t=t: e.copy(selT[:, :, t * 128:(t + 1) * 128], PTB[:, 0:768].rearrange("p (j t) -> p j t", j=6)), r=[PTB], w=[selT])
                k.op("act", lambda e: e.activation(out=kmT[:, :, c], in_=ksum[:], func=AF.Copy, scale=1.0 / 256.0), r=[ksum], w=[kmT])

                if l1stop < 6:
                    continue
                pti[0] = 0
                units = []
                for hp_ in range(6):
                    for b in [-1] + list(range(c)):
                        units.append((2 * hp_, b))
                        units.append((2 * hp_ + 1, b))
                stq = {}
                rvh = {}

                def issue_st(i):
                    h, b = units[i]
                    pb = (h % 2) * 64
                    hp = h // 2
                    qh = qT[pb:pb + 64, hp, :]
                    rs = rST.next()
                    if b < 0:
                        def mmo(e):
                            e.matmul(rs.ap[:, 0:256], kT[pb:pb + 64, hp, (2 * c) * 128:(2 * c + 1) * 128], qh, start=True, stop=False, skip_group_check=True)
                            e.matmul(rs.ap[:, 256:384], kT[pb:pb + 64, hp, (2 * c + 1) * 128:(2 * c + 2) * 128], qh[:, 128:256], start=False, stop=False, skip_group_check=True)
                            e.matmul(rs.ap[:, 0:128], identb[:], negmaskb[:], start=False, stop=False, skip_group_check=True)
                            return e.matmul(rs.ap[:, 256:384], identb[:], negmaskb[:], start=False, stop=True, skip_group_check=True)
                        k.op("pe", mmo, r=[kT, qT, identb, negmaskb], w=[rs])
                    else:
                        def mmp(e):
                            e.matmul(rs.ap[:, 0:256], kT[pb:pb + 64, hp, (2 * b) * 128:(2 * b + 1) * 128], qh, start=True, stop=False, skip_group_check=True)
                            e.matmul(rs.ap[:, 256:512], kT[pb:pb + 64, hp, (2 * b + 1) * 128:(2 * b + 2) * 128], qh, start=False, stop=False, skip_group_check=True)
                            return e.matmul(rs.ap[:, 0:512].rearrange("p (a b) -> p a b", a=2), Eoh[pb:pb + 16, b, :],
                                            selT[pb:pb + 16, hp, :].unsqueeze(1).to_broadcast([16, 2, 256]), start=False, stop=True, skip_group_check=True)
                        k.op("pe", mmp, r=[kT, qT, Eoh, selT], w=[rs])
                    stq[i] = rs

                def issue_rest(i):
                    h, b = units[i]
                    rs = stq.pop(i)
                    pt = ptv[pti[0] % NPT]
                    first_use = pti[0] < NPT
                    pti[0] += 1
                    ab = accb[h]
                    ptw = [pt, qktok] if first_use else [pt]
                    last = (b == c - 1) or (c == 0)
                    if b < 0:
                        k.op("act", lambda e: e.activation(out=pt[:, 0:384], in_=rs.ap[:, 0:384], func=AF.Exp, scale=0.125), r=[rs], w=ptw)
                        rv = rPV.next()
                        rvh[h] = rv

                        def mmpo(e):
                            e.matmul(rv.ap[:, 0:65], pt[:, 0:128], Va[:, 2 * c, h, :], start=True, stop=False, skip_group_check=True)
                            e.matmul(rv.ap[:, 65:130], pt[:, 128:256], Va[:, 2 * c, h, :], start=False, stop=False, skip_group_check=True)
                            return e.matmul(rv.ap[:, 65:130], pt[:, 256:384], Va[:, 2 * c + 1, h, :], start=False, stop=last, skip_group_check=True)
                        k.op("pe", mmpo, r=[pt, Va], w=[rv])
                        if last:
                            k.op("act", lambda e: e.copy(acc[:, :, h, :], rv.ap.rearrange("p (q d) -> p q d", q=2)), r=[rv], w=[ab])
                    else:
                        k.op("act", lambda e: e.activation(out=pt[:, :], in_=rs.ap, func=AF.Exp, scale=0.125), r=[rs], w=ptw)
                        rv = rvh[h]

                        def mmpv(e):
                            e.matmul(rv.ap[:, 0:65], pt[:, 0:128], Va[:, 2 * b, h, :], start=False, stop=False, skip_group_check=True)
                            e.matmul(rv.ap[:, 65:130], pt[:, 128:256], Va[:, 2 * b, h, :], start=False, stop=False, skip_group_check=True)
                            e.matmul(rv.ap[:, 0:65], pt[:, 256:384], Va[:, 2 * b + 1, h, :], start=False, stop=False, skip_group_check=True)
                            return e.matmul(rv.ap[:, 65:130], pt[:, 384:512], Va[:, 2 * b + 1, h, :], start=False, stop=last, skip_group_check=True)
                        k.op("pe", mmpv, r=[pt, Va], w=[rv])
                        if last:
                            k.op("act", lambda e: e.copy(acc[:, :, h, :], rv.ap.rearrange("p (q d) -> p q d", q=2)), r=[rv], w=[ab])
                nd = len(units) // 2
                issue_st(0)
                issue_st(1)
                for d in range(nd):
                    if d + 1 < nd:
                        issue_st(2 * d + 2)
                        issue_st(2 * d + 3)
                    issue_rest(2 * d)
                    issue_rest(2 * d + 1)
                if l1stop < 7:
                    continue
                rden = smalltile("rden1", [128, 2, NH], F32, n=1)
                k.op("dve", lambda e: e.reciprocal(rden[:], acc[:, :, :, HD]), r=accb, w=[rden])
                k.op("dve", lambda e: e.tensor_tensor(out=acc[:, :, :, 0:HD], in0=acc[:, :, :, 0:HD], in1=rden[:].unsqueeze(3).to_broadcast([128, 2, NH, HD]), op=ALU.mult), r=accb + [rden], w=accb)
                for t in range(2):
                    tt = 2 * c + t
                    k.op("dve", lambda e, t=t: e.tensor_tensor(out=ybf1[:, 0:768].rearrange("p (h d) -> p h d", h=NH), in0=acc[:, t, :, 0:HD],
                                                               in1=zs1[:, t, 0:768].rearrange("p (h d) -> p h d", h=NH), op=ALU.mult), r=accb + [zs1], w=[ybf1])
                    k.op("pool", lambda e, t=t: e.tensor_tensor(out=ybf1[:, 768:1024], in0=mtok1[:, t, :, :].rearrange("p h d -> p (h d)"), in1=zs1[:, t, 768:1024], op=ALU.mult), r=[mtok1, zs1], w=[ybf1])
                    out_proj(ybf1[:], [ybf1], wout1b, htok[:, t, :], [htok], htok[:, t, :], [htok])
                    ss = smalltile("fn_ss", [128, 1])
                    k.op("act", lambda e, t=t: e.activation(out=junk["ap"], in_=htok[:, t, :], func=AF.Square, accum_out=ss[:]), r=[htok], w=junk["bufs"] + [ss])
                    k.op("act", lambda e: e.activation(out=ss[:], in_=ss[:], func=AF.Ln, scale=1.0 / D, bias=EPS), r=[ss], w=[ss])
                    k.op("act", lambda e: e.activation(out=ss[:], in_=ss[:], func=AF.Exp, scale=-0.5), r=[ss], w=[ss])
                    k.op("dve", lambda e, t=t: e.scalar_tensor_tensor(out=htok[:, t, :], in0=htok[:, t, :], scalar=ss[:], in1=fnorm_bc[:], op0=ALU.mult, op1=ALU.mult), r=[htok, ss, fnorm_bc], w=[htok])
                    k.dma(out_d[tt * 128:(tt + 1) * 128, :], htok[:, t, :], r=[htok], is_output=True)
            k.barrier()
            es1.close()
            k.es = es

        k.finish()
    return nc


_CACHE = {}


def kernel(**inputs):
    S = inputs["x"].shape[1]
    B = inputs["x"].shape[0]
    key = (S,)
    if key not in _CACHE:
        _CACHE[key] = build_program(S)
    nc = _CACHE[key]
    in_maps = []
    for b in range(B):
        m = {}
        for name, v in inputs.items():
            a = np.asarray(v)
            if name in ("x", "mem", "positions"):
                a = a[b]
            m[name] = np.ascontiguousarray(a)
        in_maps.append(m)
    res = run_bass_kernel_spmd(nc, in_maps, core_ids=list(range(B)))
    return np.stack([r["out"] for r in res.results], axis=0)
```

```python
import numpy as np
from contextlib import ExitStack
import concourse.bass as bass
import concourse.mybir as mybir
from concourse.bass_utils import run_bass_kernel_spmd

F32 = mybir.dt.float32
BF16 = mybir.dt.bfloat16
F32R = mybir.dt.float32r
I32 = mybir.dt.int32
AF = mybir.ActivationFunctionType
ALU = mybir.AluOpType

D = 1024
NMEM = 256
HD = 64
NH = 12
MH = 4
DELTA_IN = 3608
MOBA_IN = 3584
EPS = 1e-6
NEG = -30000.0


class Buf:
    __slots__ = ("w", "r", "psum")

    def __init__(self, psum=False):
        self.w = None
        self.r = {}
        self.psum = psum


class T:
    def __init__(self, t, psum=False):
        self.t = t
        self.b = Buf(psum)

    def __getitem__(self, key):
        return self.t[key]


class Reg:
    def __init__(self, bank, ap):
        self.ap = ap
        self.b = bank.b


class View:
    def __init__(self, ap):
        self.ap = ap
        self.b = Buf()

    def __getitem__(self, key):
        return self.ap[key]


class Ring:
    def __init__(self, regs):
        self.regs = regs
        self.i = 0

    def next(self):
        r = self.regs[self.i % len(self.regs)]
        self.i += 1
        return r


class Eng:
    def __init__(self, name, h, sem):
        self.name = name
        self.h = h
        self.sem = sem
        self.count = 0
        self.known = {}


def _b(x):
    return x if isinstance(x, Buf) else x.b


def run_tasks(factories, nif, stagger=0, pools=None):
    pending = [f if isinstance(f, tuple) else (None, f) for f in factories]
    free = {None: list(range(nif))}
    for name, n in (pools or {}).items():
        free[name] = list(range(n))
    active = []
    since = stagger
    while active or pending:
        if pending and since >= stagger and len(active) < nif:
            for idx, (pool, fac) in enumerate(pending):
                if free[pool]:
                    slot = free[pool].pop(0)
                    pending.pop(idx)
                    active.append((fac(slot), pool, slot))
                    since = 0
                    break
        since += 1
        for item in list(active):
            g, pool, slot = item
            try:
                next(g)
            except StopIteration:
                active.remove(item)
                free[pool].append(slot)


class KB:
    def __init__(self, nc, es, n_dma_sems=12):
        self.nc = nc
        self.es = es
        self.eng = {}
        for name, h in (("pe", nc.tensor), ("act", nc.scalar), ("dve", nc.vector), ("pool", nc.gpsimd), ("sp", nc.sync)):
            sem = es.enter_context(nc.semaphore("sem_" + name))
            self.eng[name] = Eng(name, h, sem)
        self.dsems = [es.enter_context(nc.semaphore("dsem%d" % i)) for i in range(n_dma_sems)]
        self.dvals = [0] * n_dma_sems
        self.di = 0
        self.nid = 0
        self.out_tokens = []
        self.ninstr = 0
        self.alloc_log = []

    def sb(self, name, shape, dt):
        self.nid += 1
        nb = int(np.prod(shape[1:])) * (2 if dt == BF16 else 4)
        self.alloc_log.append((name, nb))
        return T(self.es.enter_context(self.nc.sbuf_tensor("%s_u%d" % (name, self.nid), list(shape), dt)))

    def ps(self, name, shape, dt):
        return T(self.es.enter_context(self.nc.psum_tensor(name, list(shape), dt)), psum=True)

    def _waits(self, E, reads, writes):
        toks = {}

        def add(tok, psum):
            if tok is None:
                return
            s, v = tok
            if s is E.sem and (psum or E.name in ("pe", "sp")):
                return
            key = id(s)
            if key not in toks or toks[key][1] < v:
                toks[key] = (s, v)

        for b in reads:
            bb = _b(b)
            add(bb.w, bb.psum)
            if bb.psum:
                for tok in bb.r.values():
                    add(tok, True)
        for b in writes:
            bb = _b(b)
            add(bb.w, bb.psum)
            for tok in bb.r.values():
                add(tok, bb.psum)
        for key, (s, v) in toks.items():
            if E.known.get(key, 0) >= v:
                continue
            E.h.wait_ge(s, v)
            self.ninstr += 1
            E.known[key] = v

    def _record(self, E, tok, reads, writes):
        for b in reads:
            bb = _b(b)
            if bb.psum:
                bb.w = tok
                bb.r = {}
            else:
                bb.r[id(tok[0])] = tok
        for b in writes:
            bb = _b(b)
            bb.w = tok
            bb.r = {}

    def op(self, eng, fn, r=(), w=()):
        E = self.eng[eng]
        self._waits(E, r, w)
        ins = fn(E.h)
        self.ninstr += 1
        E.count += 1
        ins.then_inc(E.sem, 1)
        tok = (E.sem, E.count)
        self._record(E, tok, r, w)
        return tok

    def dma(self, out, in_, r=(), w=(), queue="sp", is_output=False, **kw):
        E = self.eng[queue]
        self._waits(E, r, w)
        i = self.di % len(self.dsems)
        self.di += 1
        sem = self.dsems[i]
        if self.dvals[i] > 0 and E.known.get(id(sem), 0) < self.dvals[i]:
            E.h.wait_ge(sem, self.dvals[i])
            E.known[id(sem)] = self.dvals[i]
        ins = E.h.dma_start(out=out, in_=in_, **kw)
        self.ninstr += 1
        ins.then_inc(sem, 16)
        self.dvals[i] += 16
        tok = (sem, self.dvals[i])
        self._record(E, tok, r, w)
        if is_output:
            self.out_tokens.append(tok)
        return tok

    def barrier(self):
        toks = [(E.sem, E.count) for E in self.eng.values() if E.count > 0]
        toks += [(s, v) for s, v in zip(self.dsems, self.dvals) if v > 0]
        for E in self.eng.values():
            for s, v in toks:
                if s is E.sem:
                    continue
                if E.known.get(id(s), 0) >= v:
                    continue
                E.h.wait_ge(s, v)
                E.known[id(s)] = v

    def finish(self):
        E = self.eng["sp"]
        for s, v in self.out_tokens:
            if E.known.get(id(s), 0) < v:
                E.h.wait_ge(s, v)
                E.known[id(s)] = v


def build_program(S, do_l0=True, do_l1=True, dbg=False, l1stop=99):
    assert S % 256 == 0
    NT = S // 128
    NCH = S // 256
    nc = bass.Bass("TRN2", target_bir_lowering=False)

    def din(name, shape, dt=F32):
        return nc.dram_tensor(name, list(shape), dt, kind="ExternalInput").ap()

    x_d = din("x", [S, D])
    mem_d = din("mem", [NMEM, D])
    pos_d = din("positions", [S], I32)
    norm0_d = din("norm_0", [D])
    win0_d = din("w_in_0", [D, DELTA_IN])
    conv_d = din("conv_w_0", [4, 2304])
    alog_d = din("a_log_0", [NH])
    dtb_d = din("dt_bias_0", [NH])
    onorm_d = din("o_norm_0", [HD])
    mnorm0_d = din("mem_norm_0", [D])
    wm0_d = din("w_mem_kv_0", [D, 512])
    wout0_d = din("w_out_0", [D, D])
    norm1_d = din("norm_1", [D])
    win1_d = din("w_in_1", [D, MOBA_IN])
    mnorm1_d = din("mem_norm_1", [D])
    wm1_d = din("w_mem_kv_1", [D, 512])
    wout1_d = din("w_out_1", [D, D])
    fnorm_d = din("final_norm", [D])
    out_d = nc.dram_tensor("out", [S, D], F32, kind="ExternalOutput").ap()
    if do_l1:
        h1_d = nc.dram_tensor("h1", [S, D], F32).ap()
    else:
        h1_d = out_d
    w1b_d = nc.dram_tensor("w1b", [128, 8, MOBA_IN], BF16).ap()

    with ExitStack() as es:
        k = KB(nc, es)
        PA = [k.ps("PA%d" % i, [128, 512], F32) for i in range(2)]
        PB = k.ps("PB", [128, 512], F32)
        PTB = k.ps("PTB", [128, 1024], BF16)
        PD = [k.ps("PD%d" % i, [128, 512], F32) for i in range(4)]

        identf = k.sb("identf", [128, 128], F32)
        identb = k.sb("identb", [128, 128], BF16)
        tri = k.sb("tri", [128, 128], F32)
        cmaskb = k.sb("cmaskb", [128, 128], BF16)
        onesf = k.sb("onesf", [128, 256], F32)
        negmask = k.sb("negmask", [128, 256], F32)
        blockones = k.sb("blockones", [128, 128], F32)
        k.op("pool", lambda e: e.memset(identf[:], 1.0), w=[identf])
        k.op("pool", lambda e: e.affine_select(out=identf[:], in_=identf[:], pattern=[[-1, 128]], compare_op=ALU.is_equal,
                                               fill=0.0, base=0, channel_multiplier=1), r=[identf], w=[identf])
        k.op("dve", lambda e: e.tensor_copy(identb[:], identf[:]), r=[identf], w=[identb])
        k.op("pool", lambda e: e.memset(tri[:], 1.0), w=[tri])
        k.op("pool", lambda e: e.affine_select(out=tri[:], in_=tri[:], pattern=[[1, 128]], compare_op=ALU.is_ge,
                                               fill=0.0, base=0, channel_multiplier=-1), r=[tri], w=[tri])
        k.op("dve", lambda e: e.tensor_copy(cmaskb[:], tri[:]), r=[tri], w=[cmaskb])
        k.op("pool", lambda e: e.memset(onesf[:], 1.0), w=[onesf])
        k.op("pool", lambda e: e.memset(negmask[:], 0.0), w=[negmask])
        k.op("pool", lambda e: e.affine_select(out=negmask[:, 0:128], in_=negmask[:, 0:128], pattern=[[1, 128]], compare_op=ALU.is_ge,
                                               fill=NEG, base=0, channel_multiplier=-1), r=[negmask], w=[negmask])
        k.op("pool", lambda e: e.affine_select(out=negmask[:, 128:256], in_=negmask[:, 128:256], pattern=[[1, 128]], compare_op=ALU.is_gt,
                                               fill=NEG, base=0, channel_multiplier=-1), r=[negmask], w=[negmask])
        identr = k.sb("identr", [128, 128], F32R)
        negmaskr = k.sb("negmaskr", [128, 256], F32R)
        k.op("dve", lambda e: e.tensor_copy(identr[:], identf[:]), r=[identf], w=[identr])
        k.op("dve", lambda e: e.tensor_copy(negmaskr[:], negmask[:]), r=[negmask], w=[negmaskr])
        k.op("pool", lambda e: e.memset(blockones[:], 0.0), w=[blockones])
        k.op("pool", lambda e: e.memset(blockones[0:64, 0:64], 1.0), r=[blockones], w=[blockones])
        k.op("pool", lambda e: e.memset(blockones[64:128, 64:128], 1.0), r=[blockones], w=[blockones])

        r128 = Ring([Reg(PD[0], PD[0][:, j * 128:(j + 1) * 128]) for j in range(4)])
        r256 = Ring([Reg(bk, bk[:, j * 256:(j + 1) * 256]) for j in range(2) for bk in (PD[1], PD[2], PD[3], PB)])
        rproj = Ring([Reg(PA[i], PA[i][:, 0:256]) for i in range(2)])
        rbank = Ring([Reg(PA[i], PA[i][:, :]) for i in range(2)])

        gstage = k.sb("gstage", [32, 128], F32)
        for i, g in enumerate((norm0_d, mnorm0_d, norm1_d, mnorm1_d)):
            k.dma(gstage[i * 8:(i + 1) * 8, :], g.rearrange("(k p) -> k p", p=128), w=[gstage])
        gcols = k.sb("gcols", [128, 32], F32)
        rg = r128.next()
        k.op("pe", lambda e: e.transpose(rg.ap[:, 0:32], gstage[0:32, :], identf[0:32, 0:32]), r=[gstage, identf], w=[rg])
        k.op("dve", lambda e: e.tensor_copy(gcols[:], rg.ap[:, 0:32]), r=[rg], w=[gcols])
        cstage = k.sb("cstage", [72, 128], F32)
        k.dma(cstage[:], conv_d.rearrange("t (f p) -> (t f) p", p=128), w=[cstage])
        cw = k.sb("cw", [128, 72], F32)
        rg2 = r128.next()
        k.op("pe", lambda e: e.transpose(rg2.ap[:, 0:72], cstage[0:72, :], identf[0:72, 0:72]), r=[cstage, identf], w=[rg2])
        k.op("dve", lambda e: e.tensor_copy(cw[:], rg2.ap[:, 0:72]), r=[rg2], w=[cw])
        alog_bc = k.sb("alog_bc", [128, NH], F32)
        dtb_bc = k.sb("dtb_bc", [128, NH], F32)
        onorm_bc = k.sb("onorm_bc", [128, HD], F32)
        k.dma(alog_bc[:], alog_d.partition_broadcast(128), w=[alog_bc])
        k.dma(dtb_bc[:], dtb_d.partition_broadcast(128), w=[dtb_bc])
        k.dma(onorm_bc[:], onorm_d.partition_broadcast(128), w=[onorm_bc])
        nA_bc = k.sb("nA_bc", [128, NH], F32)
        k.op("act", lambda e: e.activation(out=nA_bc[:], in_=alog_bc[:], func=AF.Exp), r=[alog_bc], w=[nA_bc])
        k.op("dve", lambda e: e.tensor_scalar(out=nA_bc[:], in0=nA_bc[:], scalar1=-1.0, scalar2=None, op0=ALU.mult), r=[nA_bc], w=[nA_bc])

        junk = {}
        xs_ring = [k.sb("xs%d" % i, [128, D], BF16) for i in range(1)]
        xs_i = [0]
        small = {}

        def smalltile(name, shape, dt=F32, n=2):
            key = name
            if key not in small:
                small[key] = [[k.sb("%s_%d" % (name, i), shape, dt) for i in range(n)], 0]
            lst = small[key]
            t = lst[0][lst[1] % n]
            lst[1] += 1
            return t

        def norm_transpose(src, src_bufs, dstT, col0):
            ss = smalltile("nt_ss", [128, 1])
            k.op("act", lambda e: e.activation(out=junk["ap"], in_=src, func=AF.Square, accum_out=ss[:]), r=src_bufs, w=junk["bufs"] + [ss])
            k.op("act", lambda e: e.activation(out=ss[:], in_=ss[:], func=AF.Ln, scale=1.0 / D, bias=EPS), r=[ss], w=[ss])
            k.op("act", lambda e: e.activation(out=ss[:], in_=ss[:], func=AF.Exp, scale=-0.5), r=[ss], w=[ss])
            xs = xs_ring[0]
            xs_i[0] += 1
            k.op("act", lambda e: e.activation(out=xs[:], in_=src, func=AF.Copy, scale=ss[:]), r=list(src_bufs) + [ss], w=[xs])

            def tr(e):
                ins = None
                for kc in range(8):
                    ins = e.transpose(PTB[:, kc * 128:(kc + 1) * 128], xs[:, kc * 128:(kc + 1) * 128], identb[:])
                return ins
            k.op("pe", tr, r=[xs, identb], w=[PTB])
            k.op("dve", lambda e: e.tensor_copy(dstT[:, :, col0:col0 + 128], PTB[:].rearrange("p (k t) -> p k t", k=8)), r=[PTB], w=[dstT])

        ws_i = [0]
        wst = {}

        def alloc_wstage():
            wst["t"] = [k.sb("wstage%d" % i, [128, 1024], F32) for i in range(2)]

        def prep_weight(w_d, ncols, gcol0, dst_fn, dst_bufs, engs=("act", "dve"), post=None):
            for kc in range(8):
                for c0 in range(0, ncols, 1024):
                    n = min(1024, ncols - c0)
                    st = wst["t"][ws_i[0] % 2]
                    eng = engs[ws_i[0] % len(engs)]
                    ws_i[0] += 1
                    k.dma(st[:, 0:n], w_d[kc * 128:(kc + 1) * 128, c0:c0 + n], w=[st])
                    dst = dst_fn(kc, c0, n)
                    if gcol0 is None:
                        if eng == "act":
                            k.op("act", lambda e: e.copy(dst, st[:, 0:n]), r=[st], w=dst_bufs)
                        else:
                            k.op(eng, lambda e: e.tensor_copy(dst, st[:, 0:n]), r=[st], w=dst_bufs)
                    else:
                        gc = gcols[:, gcol0 + kc:gcol0 + kc + 1]
                        if eng == "act":
                            k.op("act", lambda e: e.activation(out=dst, in_=st[:, 0:n], func=AF.Copy, scale=gc), r=[st, gcols], w=dst_bufs)
                        else:
                            k.op(eng, lambda e: e.tensor_scalar(out=dst, in0=st[:, 0:n], scalar1=gc, scalar2=None, op0=ALU.mult), r=[st, gcols], w=dst_bufs)
                    if post is not None:
                        post(kc, c0, n)

        memT = k.sb("memT", [128, 8, NMEM], BF16)
        mkT = [k.sb("mkT%d" % l, [128, 2, NMEM], BF16) for l in range(2)]
        mva = [k.sb("mva%d" % l, [128, 2, MH, HD + 1], BF16) for l in range(2)]
        es_s = ExitStack()
        k.es = es_s
        alloc_wstage()
        memtok = k.sb("memtok", [128, 2, D], F32)
        sj_ = k.sb("sqjunk_s", [128, D], BF16)
        junk["ap"], junk["bufs"] = sj_[:], [sj_]
        for mt in range(2):
            k.dma(memtok[:, mt, :], mem_d[mt * 128:(mt + 1) * 128, :], w=[memtok])
        for mt in range(2):
            norm_transpose(memtok[:, mt, :], [memtok], memT, mt * 128)
        wmb = k.sb("wmb", [128, 8, 512], BF16)
        for l, (wm_d, gc0) in enumerate(((wm0_d, 8), (wm1_d, 24))):
            prep_weight(wm_d, 512, gc0, lambda kc, c0, n: wmb[:, kc, c0:c0 + n], [wmb])
            for ft in range(2):
                rr = rproj.next()

                def mm(e, ft=ft, rr=rr):
                    ins = None
                    for kc in range(8):
                        ins = e.matmul(rr.ap, wmb[:, kc, ft * 128:(ft + 1) * 128], memT[:, kc, :], start=(kc == 0), stop=(kc == 7))
                    return ins
                k.op("pe", mm, r=[wmb, memT], w=[rr])
                k.op("act", lambda e, ft=ft, rr=rr: e.copy(mkT[l][:, ft, :], rr.ap), r=[rr], w=[mkT[l]])
            k.op("pool", lambda e: e.memset(mva[l][:], 1.0), w=[mva[l]])
            for mt in range(2):
                rr = rproj.next()

                def mm2(e, mt=mt, rr=rr):
                    ins = None
                    for kc in range(8):
                        ins = e.matmul(rr.ap, memT[:, kc, mt * 128:(mt + 1) * 128], wmb[:, kc, 256:512], start=(kc == 0), stop=(kc == 7))
                    return ins
                k.op("pe", mm2, r=[wmb, memT], w=[rr])
                k.op("act", lambda e, mt=mt, rr=rr: e.copy(mva[l][:, mt, :, 0:HD], rr.ap.rearrange("p (h d) -> p h d", h=MH)), r=[rr], w=[mva[l]])

        if do_l1:
            w1st = [k.sb("w1st%d" % i, [128, 1024], BF16) for i in range(2)]
            w1i = [0]

            def w1dst(kc, c0, n):
                t_ = w1st[w1i[0] % 2]
                return t_[:, 0:n]

            def w1post(kc, c0, n):
                t_ = w1st[w1i[0] % 2]
                w1i[0] += 1
                k.dma(w1b_d[:, kc, c0:c0 + n], t_[:, 0:n], r=[t_])
            for kc_ in range(1):
                pass
            prep_weight(win1_d, MOBA_IN, 16, w1dst, [w1st[0], w1st[1]], post=w1post)
        k.barrier()
        es_s.close()
        k.es = es
        small.clear()

        def mem_attention(l, mqT, mq_bufs, ntok_tiles, m_tok):
            pm = [rbank.next() for _ in range(ntok_tiles)]
            for hm in range(MH):
                pb = (hm % 2) * 64
                for mt in range(2):
                    rs = r256.next()
                    k.op("pe", lambda e, rs=rs, mt=mt, hm=hm, pb=pb: e.matmul(rs.ap, mkT[l][pb:pb + 64, hm // 2, mt * 128:(mt + 1) * 128],
                                                                          mqT[pb:pb + 64, hm // 2, :], start=True, stop=True),
                         r=[mkT[l]] + mq_bufs, w=[rs])
                    pt = smalltile("ma_pt", [128, 256], BF16, n=(2 if l == 0 else 1))
                    k.op("act", lambda e, rs=rs, pt=pt: e.activation(out=pt[:], in_=rs.ap, func=AF.Exp, scale=0.125), r=[rs], w=[pt])
                    for t in range(ntok_tiles):
                        k.op("pe", lambda e, t=t, pt=pt, mt=mt, hm=hm: e.matmul(pm[t].ap[:, hm * 65:(hm + 1) * 65], pt[:, t * 128:(t + 1) * 128],
                                                                             mva[l][:, mt, hm, :], start=(mt == 0), stop=(mt == 1)),
                             r=[pt, mva[l]], w=[pm[t]])
                    yield
            for t in range(ntok_tiles):
                rden = smalltile("ma_rden", [128, MH])
                pv = pm[t].ap[:, 0:MH * 65].rearrange("p (h d) -> p h d", h=MH)
                k.op("dve", lambda e, pv=pv, rden=rden: e.reciprocal(rden[:], pv[:, :, 64]), r=[pm[t]], w=[rden])
                k.op("dve", lambda e, pv=pv, rden=rden, t=t: e.tensor_tensor(out=m_tok[:, t, :, :], in0=pv[:, :, 0:64],
                                                                           in1=rden[:].unsqueeze(2).to_broadcast([128, MH, HD]), op=ALU.mult),
                     r=[pm[t], rden], w=[m_tok])
            yield

        def out_proj(y, y_bufs, woutb, res_ap, res_bufs, dst, dst_bufs):
            def tr(e):
                ins = None
                for kc in range(8):
                    ins = e.transpose(PTB[:, kc * 128:(kc + 1) * 128], y[:, kc * 128:(kc + 1) * 128], identb[:])
                return ins
            k.op("pe", tr, r=list(y_bufs) + [identb], w=[PTB])
            yT = smalltile("yT", [128, 8, 128], BF16, n=1)
            k.op("act", lambda e: e.copy(yT[:].rearrange("p k t -> p (k t)"), PTB[:]), r=[PTB], w=[yT])
            for nb in range(2):
                rr = rbank.next()

                def mm(e, nb=nb, rr=rr):
                    ins = None
                    for kc in range(8):
                        ins = e.matmul(rr.ap, yT[:, kc, :], woutb[:, kc, nb * 512:(nb + 1) * 512], start=(kc == 0), stop=(kc == 7))
                    return ins
                k.op("pe", mm, r=[yT, woutb], w=[rr])
                k.op("dve", lambda e, nb=nb, rr=rr: e.tensor_tensor(out=dst[:, nb * 512:(nb + 1) * 512], in0=rr.ap, in1=res_ap[:, nb * 512:(nb + 1) * 512], op=ALU.add),
                     r=[rr] + list(res_bufs), w=dst_bufs)

        if do_l0:
            es0 = ExitStack()
            k.es = es0
            w0b = k.sb("w0b", [128, 8, DELTA_IN], BF16)
            wout0b = k.sb("wout0b", [128, 8, D], BF16)
            es0s = ExitStack()
            k.es = es0s
            alloc_wstage()
            prep_weight(win0_d, DELTA_IN, 0, lambda kc, c0, n: w0b[:, kc, c0:c0 + n], [w0b])
            prep_weight(wout0_d, D, None, lambda kc, c0, n: wout0b[:, kc, c0:c0 + n], [wout0b])
            k.barrier()
            es0s.close()
            k.es = es0

            xtok = [k.sb("xtok%d" % i, [128, 2, D], F32) for i in range(1)]
            xT = [k.sb("xT%d" % i, [128, 8, 256], BF16) for i in range(1)]
            halo = k.sb("halo", [128, 18, 3], F32)
            halob = [Buf() for _ in range(18)]
            k.op("pool", lambda e: e.memset(halo[:], 0.0), w=halob)
            qkT = k.sb("qkT", [128, 12, 256], F32)
            ktok = k.sb("ktok", [128, 2, 768], F32)
            vtok = k.sb("vtok", [128, 2, 768], F32)
            zs = k.sb("zs", [128, 2, D], BF16)
            mqT = k.sb("mqT", [128, 2, 256], BF16)
            Sst = k.sb("Sst", [128, 6, 128], F32)
            k.op("pool", lambda e: e.memset(Sst[:], 0.0), w=[Sst])
            NIFP = 2
            QS = [{"raw": k.sb("raw%d" % i, [128, 259], F32), "acc": k.sb("cacc%d" % i, [128, 256], F32), "sl": k.sb("sl%d" % i, [128, 256], F32)} for i in range(NIFP)]
            Q2S = [{"sq": k.sb("sq%d" % i, [128, 256], F32), "rn": k.sb("rn%d" % i, [128, 256], F32)} for i in range(1)]
            qkb = [Buf() for _ in range(12)]
            ktb = [Buf() for _ in range(6)]
            vtb = [Buf() for _ in range(6)]
            NIF = 4
            PS = []
            for s_ in range(NIF):
                P = {"h": []}
                for hl in range(2):
                    H = {"XR": [k.sb("XR%d_%d_0" % (s_, hl), [128, 256], F32R), k.sb("XR%d_%d_1" % (s_, hl), [128, 256], F32R)],
                         "YY": k.sb("YY%d_%d" % (s_, hl), [128, 384], F32R),
                         "AT": k.sb("AT%d_%d" % (s_, hl), [128, 128], F32)}
                    for xr_ in H["XR"]:
                        xr_.bx = Buf()
                        xr_.br = Buf()
                    k.op("dve", lambda e, H=H: e.tensor_copy(H["XR"][0][:], onesf[:, 0:256]), r=[onesf], w=[H["XR"][0].bx, H["XR"][0].br])
                    k.op("dve", lambda e, H=H: e.tensor_copy(H["XR"][1][:], onesf[:, 0:256]), r=[onesf], w=[H["XR"][1].bx, H["XR"][1].br])
                    k.op("dve", lambda e, H=H: e.tensor_copy(H["YY"][:, 0:256], onesf[:, 0:256]), r=[onesf], w=[H["YY"]])
                    k.op("dve", lambda e, H=H: e.tensor_copy(H["YY"][:, 256:384], onesf[:, 0:128]), r=[onesf], w=[H["YY"]])
                    P["h"].append(H)
                for nm in ("vb", "kbg", "kdec", "nwT", "vnew", "o2", "opair", "osq"):
                    P[nm] = k.sb("%s%d" % (nm, s_), [128, 128], F32)
                P["oss"] = k.sb("oss%d" % s_, [128, 2], F32)
                PS.append(P)
            ybfb = [Buf() for _ in range(6)]
            mtok = k.sb("mtok", [128, 2, MH, HD], F32)
            ybf = k.sb("ybf", [128, D], BF16)
            junk["ap"], junk["bufs"] = ybf[:], [ybf] + ybfb

            for c in range(NCH):
                xt = xtok[0]
                xTc = xT[0]
                for t in range(2):
                    tt = 2 * c + t
                    k.dma(xt[:, t, :], x_d[tt * 128:(tt + 1) * 128, :], w=[xt])
                for t in range(2):
                    norm_transpose(xt[:, t, :], [xt], xTc, t * 128)

                def ft_factory(ft):
                    def fac(slot):
                        return ft_task(ft, QS[slot])
                    return fac

                def ft_task(ft, Q):
                    raw, acc = Q["raw"], Q["acc"]
                    rr = rproj.next()

                    def mm(e):
                        ins = None
                        for kc in range(8):
                            ins = e.matmul(rr.ap, w0b[:, kc, ft * 128:(ft + 1) * 128], xTc[:, kc, :], start=(kc == 0), stop=(kc == 7))
                        return ins
                    k.op("pe", mm, r=[w0b, xTc], w=[rr])
                    k.op("pool", lambda e: e.tensor_copy(raw[:, 0:3], halo[:, ft, :]), r=[halob[ft]], w=[raw])
                    k.op("act", lambda e: e.copy(raw[:, 3:259], rr.ap), r=[rr], w=[raw])
                    k.op("pool", lambda e: e.tensor_copy(halo[:, ft, :], raw[:, 256:259]), r=[raw], w=[halob[ft]])
                    yield
                    k.op("dve", lambda e: e.tensor_scalar(out=acc[:], in0=raw[:, 0:256], scalar1=cw[:, ft:ft + 1], scalar2=None, op0=ALU.mult), r=[raw, cw], w=[acc])
                    for tap in range(1, 4):
                        k.op("dve", lambda e, tap=tap: e.scalar_tensor_tensor(out=acc[:], in0=raw[:, tap:tap + 256], scalar=cw[:, tap * 18 + ft:tap * 18 + ft + 1], in1=acc[:],
                                                                         op0=ALU.mult, op1=ALU.add), r=[raw, cw, acc], w=[acc])
                    yield
                    if ft < 12:
                        k.op("act", lambda e: e.activation(out=qkT[:, ft, :].bitcast(F32R), in_=acc[:], func=AF.Silu), r=[acc], w=[qkb[ft]])
                    else:
                        sl = Q["sl"]
                        fo = ft - 12
                        k.op("act", lambda e: e.activation(out=sl[:], in_=acc[:], func=AF.Silu), r=[acc], w=[sl])
                        yield
                        for t in range(2):
                            rt = r128.next()
                            k.op("pe", lambda e, t=t, rt=rt: e.transpose(rt.ap, sl[:, t * 128:(t + 1) * 128], identf[:]), r=[sl, identf], w=[rt])
                            if t == 0:
                                k.op("act", lambda e, t=t, rt=rt: e.copy(vtok[:, t, fo * 128:(fo + 1) * 128], rt.ap), r=[rt], w=[vtb[fo]])
                            else:
                                k.op("dve", lambda e, t=t, rt=rt: e.tensor_copy(vtok[:, t, fo * 128:(fo + 1) * 128], rt.ap), r=[rt], w=[vtb[fo]])
                    yield

                run_tasks([ft_factory(ft) for ft in range(18)], NIFP, stagger=1)
                for ft in range(2):
                    rr = rproj.next()
                    c0 = 2304 + 1024 + ft * 128

                    def mm(e, c0=c0, rr=rr):
                        ins = None
                        for kc in range(8):
                            ins = e.matmul(rr.ap, w0b[:, kc, c0:c0 + 128], xTc[:, kc, :], start=(kc == 0), stop=(kc == 7))
                        return ins
                    k.op("pe", mm, r=[w0b, xTc], w=[rr])
                    k.op("act", lambda e, ft=ft, rr=rr: e.copy(mqT[:, ft, :], rr.ap), r=[rr], w=[mqT])
                for t in range(2):
                    for nb in range(2):
                        rr = rbank.next()
                        c0 = 2304 + nb * 512

                        def mm(e, c0=c0, rr=rr, t=t):
                            ins = None
                            for kc in range(8):
                                ins = e.matmul(rr.ap, xTc[:, kc, t * 128:(t + 1) * 128], w0b[:, kc, c0:c0 + 512], start=(kc == 0), stop=(kc == 7))
                            return ins
                        k.op("pe", mm, r=[w0b, xTc], w=[rr])
                        k.op("act", lambda e, t=t, nb=nb, rr=rr: e.activation(out=zs[:, t, nb * 512:(nb + 1) * 512], in_=rr.ap, func=AF.Silu), r=[rr], w=[zs])

                def l2_factory(ft):
                    def fac(slot):
                        return l2_task(ft, Q2S[slot])
                    return fac

                def l2_task(ft, Q2):
                    sq, rn = Q2["sq"], Q2["rn"]
                    k.op("act", lambda e: e.activation(out=sq[:], in_=qkT[:, ft, :], func=AF.Square), r=[qkb[ft]], w=[sq])
                    yield
                    rs = r256.next()
                    k.op("pe", lambda e: e.matmul(rs.ap, blockones[:], sq[:], start=True, stop=True), r=[blockones, sq], w=[rs])
                    k.op("act", lambda e: e.activation(out=rn[:], in_=rs.ap, func=AF.Ln, bias=EPS), r=[rs], w=[rn])
                    yield
                    k.op("act", lambda e: e.activation(out=rn[:], in_=rn[:], func=AF.Exp, scale=-0.5), r=[rn], w=[rn])
                    yield
                    sc = 0.125 if ft < 6 else 1.0
                    k.op("dve", lambda e: e.scalar_tensor_tensor(out=qkT[:, ft, :].bitcast(F32R), in0=qkT[:, ft, :], scalar=sc, in1=rn[:], op0=ALU.mult, op1=ALU.mult),
                         r=[qkb[ft], rn], w=[qkb[ft]])
                    if ft >= 6:
                        yield
                        fo = ft - 6
                        for t in range(2):
                            rt = r128.next()
                            k.op("pe", lambda e, t=t, rt=rt: e.transpose(rt.ap, qkT[:, ft, t * 128:(t + 1) * 128], identf[:]), r=[qkb[ft], identf], w=[rt])
                            if t == 0:
                                k.op("act", lambda e, t=t, rt=rt: e.copy(ktok[:, t, fo * 128:(fo + 1) * 128], rt.ap), r=[rt], w=[ktb[fo]])
                            else:
                                k.op("dve", lambda e, t=t, rt=rt: e.tensor_copy(ktok[:, t, fo * 128:(fo + 1) * 128], rt.ap), r=[rt], w=[ktb[fo]])
                    yield


                Gt = [smalltile("gates", [128, 12, NH], F32, n=2) for _ in range(2)]

                def gate_task(t):
                    G = Gt[t]
                    gb_ = [G]
                    rgate = r128.next()

                    def mmg(e):
                        ins = None
                        for kc in range(8):
                            ins = e.matmul(rgate.ap[:, 0:24], xTc[:, kc, t * 128:(t + 1) * 128], w0b[:, kc, 3584:3608], start=(kc == 0), stop=(kc == 7))
                        return ins
                    k.op("pe", mmg, r=[w0b, xTc], w=[rgate])
                    k.op("dve", lambda e: e.tensor_tensor(out=G[:, 0, :], in0=rgate.ap[:, 12:24], in1=dtb_bc[:], op=ALU.add), r=[rgate, dtb_bc], w=gb_)
                    k.op("act", lambda e: e.activation(out=G[:, 1, :], in_=rgate.ap[:, 0:12], func=AF.Exp, scale=-1.0), r=[rgate], w=gb_)
                    yield
                    k.op("act", lambda e: e.activation(out=G[:, 0, :], in_=G[:, 0, :], func=AF.Exp), r=gb_, w=gb_)
                    k.op("dve", lambda e: e.tensor_scalar(out=G[:, 1, :], in0=G[:, 1, :], scalar1=1.0, scalar2=None, op0=ALU.add), r=gb_, w=gb_)
                    yield
                    k.op("act", lambda e: e.activation(out=G[:, 0, :], in_=G[:, 0, :], func=AF.Ln, bias=1.0), r=gb_, w=gb_)
                    k.op("dve", lambda e: e.reciprocal(G[:, 2, :], G[:, 1, :]), r=gb_, w=gb_)
                    yield
                    k.op("dve", lambda e: e.tensor_tensor(out=G[:, 0, :], in0=G[:, 0, :], in1=nA_bc[:], op=ALU.mult), r=gb_ + [nA_bc], w=gb_)
                    k.op("act", lambda e: e.activation(out=G[:, 3, :], in_=G[:, 1, :], func=AF.Ln), r=gb_, w=gb_)
                    yield
                    rgc = r128.next()
                    k.op("pe", lambda e: e.matmul(rgc.ap[:, 0:NH], tri[:], G[:, 0, :], start=True, stop=True), r=[tri] + gb_, w=[rgc])
                    k.op("pe", lambda e: e.matmul(rgc.ap[:, 16:16 + NH], onesf[:, 0:128], G[:, 0, :], start=True, stop=True), r=[onesf] + gb_, w=[rgc])
                    k.op("dve", lambda e: e.tensor_copy(G[:, 4, :], rgc.ap[:, 0:NH]), r=[rgc], w=gb_)
                    k.op("act", lambda e: e.activation(out=G[:, 5, :], in_=rgc.ap[:, 0:NH], func=AF.Exp), r=[rgc], w=gb_)
                    k.op("act", lambda e: e.activation(out=G[:, 8, :], in_=rgc.ap[:, 16:16 + NH], func=AF.Exp), r=[rgc], w=gb_)
                    k.op("dve", lambda e: e.tensor_tensor(out=G[:, 7, :], in0=rgc.ap[:, 16:16 + NH], in1=G[:, 4, :], op=ALU.subtract), r=[rgc] + gb_, w=gb_)
                    yield
                    k.op("dve", lambda e: e.tensor_tensor(out=G[:, 6, :], in0=G[:, 2, :], in1=G[:, 5, :], op=ALU.mult), r=gb_, w=gb_)
                    k.op("act", lambda e: e.activation(out=G[:, 7, :], in_=G[:, 7, :], func=AF.Exp), r=gb_, w=gb_)
                    k.op("dve", lambda e: e.tensor_tensor(out=G[:, 9, :], in0=G[:, 4, :], in1=G[:, 3, :], op=ALU.subtract), r=gb_, w=gb_)
                    k.op("dve", lambda e: e.tensor_scalar(out=G[:, 10, :], in0=G[:, 4, :], scalar1=-1.0, scalar2=None, op0=ALU.mult), r=gb_, w=gb_)
                    yield

                run_tasks([lambda slot: gate_task(0), lambda slot: gate_task(1), lambda slot: mem_attention(0, mqT, [mqT], 2, mtok)]
                          + [("l2", l2_factory(ft)) for ft in range(12)], 5, stagger=0, pools={"l2": 1})

                for t in range(2):
                    ts = slice(t * 128, (t + 1) * 128)
                    G = Gt[t]
                    gb_ = [G]
                    tt = 2 * c + t
                    yT = smalltile("yT", [128, 8, 128], BF16, n=1)
                    k.op("dve", lambda e: e.tensor_tensor(out=ybf[:, 768:1024], in0=mtok[:, t, :, :].rearrange("p h d -> p (h d)"), in1=zs[:, t, 768:1024], op=ALU.mult),
                         r=[mtok, zs], w=[ybf])

                    def trm(e):
                        e.transpose(PTB[:, 768:896], ybf[:, 768:896], identb[:])
                        return e.transpose(PTB[:, 896:1024], ybf[:, 896:1024], identb[:])
                    k.op("pe", trm, r=[ybf, identb], w=[PTB])
                    k.op("act", lambda e: e.copy(yT[:, 6:8, :], PTB[:, 768:1024].rearrange("p (k t) -> p k t", k=2)), r=[PTB], w=[yT])
                    pacc = [rbank.next(), rbank.next()]
                    opn = [0]

                    def oproj(kcs, last):
                        for nb in range(2):
                            def mm(e, nb=nb):
                                ins = None
                                for i_, kc in enumerate(kcs):
                                    first = (opn[0] == 0 and i_ == 0)
                                    ins = e.matmul(pacc[nb].ap, yT[:, kc, :], wout0b[:, kc, nb * 512:(nb + 1) * 512], start=first, stop=(last and i_ == len(kcs) - 1),
                                                   skip_group_check=True)
                                return ins
                            k.op("pe", mm, r=[yT, wout0b], w=[pacc[nb]])
                        opn[0] += 1
                    oproj([6, 7], False)
                    npair_done = [0]

                    def pair_task(hp, P):
                        cs = slice(hp * 128, (hp + 1) * 128)
                        k3p = ktok[:, t, cs].rearrange("p (h d) -> p h d", h=2)
                        v3p = vtok[:, t, cs].rearrange("p (h d) -> p h d", h=2)

                        def bcp(slot):
                            return G[:, slot, 2 * hp:2 * hp + 2].unsqueeze(2).to_broadcast([128, 2, HD])
                        vb, kbg, kdec = P["vb"], P["kbg"], P["kdec"]
                        k.op("dve", lambda e: e.tensor_tensor(out=vb[:].rearrange("p (h d) -> p h d", h=2), in0=v3p, in1=bcp(2), op=ALU.mult), r=[vtb[hp]] + gb_, w=[vb])
                        k.op("dve", lambda e: e.tensor_tensor(out=kbg[:].rearrange("p (h d) -> p h d", h=2), in0=k3p, in1=bcp(6), op=ALU.mult), r=[ktb[hp]] + gb_, w=[kbg])
                        for hl_ in range(2):
                            k.op("act", lambda e, hl_=hl_: e.activation(out=kdec[:, hl_ * 64:(hl_ + 1) * 64], in_=ktok[:, t, hp * 128 + hl_ * 64:hp * 128 + (hl_ + 1) * 64], func=AF.Copy,
                                                                     scale=G[:, 7, 2 * hp + hl_:2 * hp + hl_ + 1]), r=[ktb[hp]] + gb_, w=[kdec])
                        for hl in range(2):
                            h = 2 * hp + hl
                            pb = hl * 64
                            H = P["h"][hl]
                            GG = smalltile("GG", [128, 256], F32, n=2)
                            k.op("act", lambda e: e.activation(out=GG[:, 0:128], in_=identf[:], func=AF.Copy, scale=G[:, 4, h:h + 1]), r=[identf] + gb_, w=[GG])
                            k.op("act", lambda e: e.activation(out=GG[:, 128:256], in_=identf[:], func=AF.Copy, scale=G[:, 9, h:h + 1]), r=[identf] + gb_, w=[GG])
                            rD = r256.next()

                            def mmD(e):
                                e.matmul(rD.ap, onesf[:, 0:128], GG[:], start=True, stop=False)
                                return e.matmul(rD.ap, identr[:], negmaskr[:], start=False, stop=True)
                            k.op("pe", mmD, r=[onesf, GG, identr, negmaskr], w=[rD])
                            EE = smalltile("EE", [128, 256], F32, n=2)
                            k.op("act", lambda e: e.activation(out=EE[:], in_=rD.ap, func=AF.Exp, bias=G[:, 10, h:h + 1]), r=[rD] + gb_, w=[EE])
                            rK = r256.next()

                            def mmK(e):
                                return e.matmul(rK.ap.rearrange("p (a b) -> p a b", a=2), qkT[pb:pb + 64, 6 + hp, ts].bitcast(F32R),
                                                qkT[pb:pb + 64, hp:12:6, ts].bitcast(F32R), start=True, stop=True)
                            k.op("pe", mmK, r=[qkb[hp], qkb[6 + hp]], w=[rK])
                            XR0, XR1, YY = H["XR"][0], H["XR"][1], H["YY"]
                            k.op("dve", lambda e: e.tensor_tensor(out=H["AT"][:], in0=rK.ap[:, 0:128], in1=EE[:, 0:128], op=ALU.mult), r=[rK, EE], w=[H["AT"]])
                            k.op("dve", lambda e: e.tensor_tensor(out=XR0[:, 0:128], in0=rK.ap[:, 128:256], in1=EE[:, 128:256], op=ALU.mult), r=[rK, EE], w=[XR0.bx])
                            rY = r128.next()
                            k.op("pe", lambda e: e.transpose(rY.ap.bitcast(F32R), XR0[:, 0:128], identr[:]), r=[XR0.bx, identr], w=[rY])
                            k.op("act", lambda e: e.copy(YY[:, 0:128], rY.ap), r=[rY], w=[YY])
                            k.op("dve", lambda e: e.tensor_tensor(out=XR1[:, 128:256], in0=identf[:], in1=XR0[:, 0:128], op=ALU.subtract), r=[identf, XR0.bx], w=[XR1.br])
                        yield
                        for hl in range(2):
                            H = P["h"][hl]
                            XR0, XR1, YY = H["XR"][0], H["XR"][1], H["YY"]
                            rA = r256.next()
                            k.op("pe", lambda e: e.matmul(rA.ap, YY[:, 0:128], XR0[:, :], start=True, stop=True), r=[YY, XR0.bx, XR0.br], w=[rA])
                            k.op("act", lambda e: e.copy(XR1[:, 0:128], rA.ap[:, 0:128]), r=[rA], w=[XR1.bx])
                            rB = r256.next()
                            k.op("pe", lambda e: e.matmul(rB.ap, XR0[:, 0:128], YY[:, 0:256], start=True, stop=True), r=[YY, XR0.bx], w=[rB])
                            k.op("dve", lambda e: e.tensor_copy(YY[:, 128:256], rB.ap[:, 0:128]), r=[rB], w=[YY])
                        yield
                        for lev in range(1, 7):
                            for hl in range(2):
                                H = P["h"][hl]
                                XRj, XRn, YY = H["XR"][lev % 2], H["XR"][(lev + 1) % 2], H["YY"]
                                ys = (lev % 2) * 128
                                yn = ((lev + 1) % 2) * 128
                                rA = r256.next()
                                k.op("pe", lambda e: e.matmul(rA.ap, YY[:, ys:ys + 128], XRj[:, :], start=True, stop=True), r=[YY, XRj.bx, XRj.br], w=[rA])
                                if lev < 6:
                                    k.op("act", lambda e: e.copy(XRn[:, 0:128], rA.ap[:, 0:128]), r=[rA], w=[XRn.bx])
                                    k.op("dve", lambda e: e.tensor_tensor(out=XRn[:, 128:256], in0=rA.ap[:, 128:256], in1=XRj[:, 128:256], op=ALU.add), r=[rA, XRj.br], w=[XRn.br])
                                    rB = r256.next()
                                    k.op("pe", lambda e: e.matmul(rB.ap, XRj[:, 0:128], YY[:, ys:ys + 256], start=True, stop=True), r=[YY, XRj.bx], w=[rB])
                                    k.op("dve", lambda e: e.tensor_copy(YY[:, yn:yn + 128], rB.ap[:, 0:128]), r=[rB], w=[YY])
                                else:
                                    k.op("dve", lambda e: e.tensor_tensor(out=XRn[:, 128:256], in0=rA.ap[:, 128:256], in1=XRj[:, 128:256], op=ALU.add), r=[rA, XRj.br], w=[XRn.br])
                            yield
                        Rs = [View(P["h"][0]["XR"][1][:, 128:256].bitcast(F32)), View(P["h"][1]["XR"][1][:, 128:256].bitcast(F32))]
                        Rs[0].b = P["h"][0]["XR"][1].br
                        Rs[1].b = P["h"][1]["XR"][1].br
                        ATs = [P["h"][0]["AT"], P["h"][1]["AT"]]
                        nwT, vnew, o2, opair = P["nwT"], P["vnew"], P["o2"], P["opair"]
                        for hl in range(2):
                            rW = r128.next()
                            k.op("pe", lambda e: e.matmul(rW.ap, kbg[:], Rs[hl][:], start=True, stop=True), r=[kbg, Rs[hl]], w=[rW])
                            pb = hl * 64
                            k.op("act", lambda e: e.activation(out=nwT[pb:pb + 64, :], in_=rW.ap[pb:pb + 64, :], func=AF.Copy, scale=-1.0), r=[rW], w=[nwT])
                        yield
                        rV = r128.next()

                        def mmV(e):
                            for hl in range(2):
                                e.matmul(rV.ap[:, hl * 64:(hl + 1) * 64], Rs[hl][:], vb[:, hl * 64:(hl + 1) * 64], start=(hl == 0), stop=False)
                            return e.matmul(rV.ap, nwT[:], Sst[:, hp, :], start=False, stop=True)
                        k.op("pe", mmV, r=[Rs[0], Rs[1], vb, nwT, Sst], w=[rV])
                        k.op("act", lambda e: e.copy(vnew[:], rV.ap), r=[rV], w=[vnew])
                        yield
                        rO1 = r128.next()
                        k.op("pe", lambda e: e.matmul(rO1.ap, qkT[:, hp, ts], Sst[:, hp, :], start=True, stop=True), r=[qkb[hp], Sst], w=[rO1])
                        rO2 = r128.next()

                        def mmO2(e):
                            ins = None
                            for hl in range(2):
                                ins = e.matmul(rO2.ap[:, hl * 64:(hl + 1) * 64], ATs[hl][:], vnew[:, hl * 64:(hl + 1) * 64], start=True, stop=True)
                            return ins
                        k.op("pe", mmO2, r=[ATs[0], ATs[1], vnew], w=[rO2])
                        k.op("act", lambda e: e.copy(o2[:], rO2.ap), r=[rO2], w=[o2])
                        for hl in range(2):
                            h = 2 * hp + hl
                            k.op("dve", lambda e: e.scalar_tensor_tensor(out=opair[:, hl * 64:(hl + 1) * 64], in0=rO1.ap[:, hl * 64:(hl + 1) * 64], scalar=G[:, 5, h:h + 1],
                                                                        in1=o2[:, hl * 64:(hl + 1) * 64], op0=ALU.mult, op1=ALU.add),
                                 r=[rO1, o2] + gb_, w=[opair])
                        yield
                        rS = r128.next()
                        k.op("pe", lambda e: e.matmul(rS.ap, kdec[:], vnew[:], start=True, stop=True), r=[kdec, vnew], w=[rS])
                        for hl in range(2):
                            h = 2 * hp + hl
                            pb = hl * 64
                            k.op("dve", lambda e: e.scalar_tensor_tensor(out=Sst[pb:pb + 64, hp, pb:pb + 64], in0=Sst[pb:pb + 64, hp, pb:pb + 64],
                                                                        scalar=G[pb:pb + 64, 8, h:h + 1], in1=rS.ap[pb:pb + 64, pb:pb + 64], op0=ALU.mult, op1=ALU.add),
                                 r=[rS, Sst] + gb_, w=[Sst])
                        osq, oss = P["osq"], P["oss"]
                        k.op("act", lambda e: e.activation(out=osq[:], in_=opair[:], func=AF.Square), r=[opair], w=[osq])
                        k.op("dve", lambda e: e.tensor_reduce(out=oss[:], in_=osq[:].rearrange("p (h d) -> p h d", h=2), axis=mybir.AxisListType.X, op=ALU.add), r=[osq], w=[oss])
                        yield
                        k.op("act", lambda e: e.activation(out=oss[:], in_=oss[:], func=AF.Ln, scale=1.0 / HD, bias=EPS), r=[oss], w=[oss])
                        k.op("act", lambda e: e.activation(out=oss[:], in_=oss[:], func=AF.Exp, scale=-0.5), r=[oss], w=[oss])
                        for hl in range(2):
                            k.op("dve", lambda e: e.scalar_tensor_tensor(out=osq[:, hl * 64:(hl + 1) * 64], in0=opair[:, hl * 64:(hl + 1) * 64], scalar=oss[:, hl:hl + 1],
                                                                        in1=onorm_bc[:], op0=ALU.mult, op1=ALU.mult), r=[opair, oss, onorm_bc, osq], w=[osq])
                        k.op("dve", lambda e: e.tensor_tensor(out=ybf[:, cs], in0=osq[:], in1=zs[:, t, cs], op=ALU.mult), r=[osq, zs], w=[ybfb[hp]])
                        yield
                        k.op("pe", lambda e: e.transpose(PTB[:, cs], ybf[:, cs], identb[:]), r=[ybfb[hp], identb], w=[PTB])
                        k.op("act", lambda e: e.copy(yT[:, hp, :], PTB[:, cs]), r=[PTB], w=[yT])
                        npair_done[0] += 1
                        oproj([hp], npair_done[0] == 6)
                        yield

                    def pair_factory(hp):
                        def fac(slot):
                            return pair_task(hp, PS[slot])
                        return fac
                    run_tasks([pair_factory(hp) for hp in range(6)], NIF, stagger=6)
                    for nb in range(2):
                        k.op("dve", lambda e, nb=nb: e.tensor_tensor(out=xt[:, t, nb * 512:(nb + 1) * 512], in0=pacc[nb].ap, in1=xt[:, t, nb * 512:(nb + 1) * 512], op=ALU.add),
                             r=[pacc[nb], xt], w=[xt])
                    k.dma(h1_d[tt * 128:(tt + 1) * 128, :], xt[:, t, :], r=[xt], is_output=(not do_l1))
            k.barrier()
            es0.close()
            k.es = es
            small.clear()

        if do_l1:
            import math
            h1src = h1_d if do_l0 else x_d
            es1 = ExitStack()
            k.es = es1
            wout1b = k.sb("wout1b", [128, 8, D], BF16)
            posf = k.sb("posf", [128, NT], F32)
            es1s = ExitStack()
            k.es = es1s
            alloc_wstage()
            prep_weight(wout1_d, D, None, lambda kc, c0, n: wout1b[:, kc, c0:c0 + n], [wout1b])
            pst_i = k.sb("pst_i", [NT, 128], I32)
            pst_f = k.sb("pst_f", [NT, 128], F32)
            k.dma(pst_i[:], pos_d.rearrange("(t p) -> t p", p=128), w=[pst_i])
            k.op("dve", lambda e: e.tensor_copy(pst_f[:], pst_i[:]), r=[pst_i], w=[pst_f])
            rp = r128.next()
            k.op("pe", lambda e: e.transpose(rp.ap[:, 0:NT], pst_f[0:NT, :], identf[0:NT, 0:NT]), r=[pst_f, identf], w=[rp])
            k.op("dve", lambda e: e.tensor_copy(posf[:], rp.ap[:, 0:NT]), r=[rp], w=[posf])
            k.barrier()
            es1s.close()
            k.es = es1
            small.clear()

            fnorm_bc = k.sb("fnorm_bc", [128, D], F32)
            k.dma(fnorm_bc[:], fnorm_d.partition_broadcast(128), w=[fnorm_bc])
            invf = k.sb("invf", [128, 32], F32)
            for i in range(32):
                k.op("pool", lambda e, i=i: e.memset(invf[:, i:i + 1], float(np.float32(10000.0) ** np.float32(-i / 32.0))), w=[invf])
            kT = k.sb("kT", [128, 6, S], BF16)
            Va = k.sb("Va", [128, NT, NH, HD + 1], BF16)
            k.op("pool", lambda e: e.memset(Va[:], 1.0), w=[Va])
            kmT = k.sb("kmT", [128, 6, 16], BF16)
            k.op("pool", lambda e: e.memset(kmT[:], 0.0), w=[kmT])
            htok = k.sb("htok", [128, 2, D], F32)
            hT = k.sb("hT", [128, 8, 256], BF16)
            wblk = [k.sb("wblk%d" % i, [128, 8, 256], BF16) for i in range(2)]
            qktok = k.sb("qktok", [128, 2, 1536], F32)
            qkrot = k.sb("qkrot", [128, 24, HD], BF16)
            qT = k.sb("qT", [128, 6, 256], BF16)
            zs1 = k.sb("zs1", [128, 2, D], BF16)
            mqT1 = k.sb("mqT1", [128, 2, 256], BF16)
            cs = k.sb("cs", [128, 2, 2, 32], F32)
            sel = k.sb("sel", [128, 2, NH, 16], F32)
            acc = k.sb("acc", [128, 2, NH, HD + 1], F32)
            mtok1 = k.sb("mtok1", [128, 2, MH, HD], F32)
            ybf1 = k.sb("ybf1", [128, D], BF16)
            junk["ap"], junk["bufs"] = ybf1[:], [ybf1]
            rST = Ring([Reg(PB, PB[:, :]), Reg(PD[3], PD[3][:, :]), Reg(PA[0], PA[0][:, :]), Reg(PA[1], PA[1][:, :])])
            rPV = Ring([Reg(PD[0], PD[0][:, 0:130]), Reg(PD[1], PD[1][:, 0:130]), Reg(PD[2], PD[2][:, 0:130])])
            accb = [Buf() for _ in range(NH)]
            NPT = 5
            ptv = [View(qktok[:, 0, j * 256:(j + 1) * 256].bitcast(BF16)) for j in range(NPT)]
            pti = [0]
            Eoh = k.sb("Eoh", [128, 16, 128], BF16)
            k.op("pool", lambda e: e.memset(Eoh[:], 1.0), w=[Eoh])
            for half in range(2):
                k.op("pool", lambda e, half=half: e.affine_select(out=Eoh[half * 64:(half + 1) * 64], in_=Eoh[half * 64:(half + 1) * 64], pattern=[[-1, 16], [0, 128]],
                                                             compare_op=ALU.is_equal, fill=0.0, base=0, channel_multiplier=1), r=[Eoh], w=[Eoh])
            selst = k.sb("selst", [128, 6, 128], BF16)
            k.op("pool", lambda e: e.memset(selst[:], 0.0), w=[selst])
            selT = k.sb("selT", [128, 6, 256], BF16)
            negmaskb = k.sb("negmaskb", [128, 128], BF16)
            k.op("dve", lambda e: e.tensor_copy(negmaskb[:], negmask[:, 0:128]), r=[negmask], w=[negmaskb])
            rGe = Reg(PD[2], PD[2][:, 0:96])
            rGo = Reg(PA[0], PA[0][:, 0:96])
            TWO_PI = 2.0 * math.pi
            C1 = 6.28125
            C2 = TWO_PI - C1
            MAGIC = 12582912.0
            wbi = [0]

            for c in range(NCH):
                if l1stop < 1:
                    continue
                for t in range(2):
                    tt = 2 * c + t
                    k.dma(htok[:, t, :], h1src[tt * 128:(tt + 1) * 128, :], w=[htok])
                for t in range(2):
                    norm_transpose(htok[:, t, :], [htok], hT, t * 128)
                a4 = smalltile("a4", [128, 2, 2, 32], F32, n=1)
                v4 = smalltile("v4", [128, 2, 2, 32], F32, n=1)
                w4 = smalltile("w4", [128, 2, 2, 32], F32, n=1)
                for t in range(2):
                    tt = 2 * c + t
                    k.op("dve", lambda e, t=t, tt=tt: e.tensor_scalar(out=a4[:, t, 1, :], in0=invf[:], scalar1=posf[:, tt:tt + 1], scalar2=None, op0=ALU.mult), r=[invf, posf], w=[a4])
                    k.op("dve", lambda e, t=t: e.tensor_scalar(out=a4[:, t, 0, :], in0=a4[:, t, 1, :], scalar1=math.pi / 2, scalar2=None, op0=ALU.add), r=[a4], w=[a4])
                k.op("dve", lambda e: e.tensor_scalar(out=v4[:], in0=a4[:], scalar1=1.0 / TWO_PI, scalar2=MAGIC, op0=ALU.mult, op1=ALU.add), r=[a4], w=[v4])
                k.op("dve", lambda e: e.tensor_scalar(out=w4[:], in0=v4[:], scalar1=MAGIC, scalar2=-C1, op0=ALU.subtract, op1=ALU.mult), r=[v4], w=[w4])
                k.op("dve", lambda e: e.tensor_tensor(out=a4[:], in0=a4[:], in1=w4[:], op=ALU.add), r=[a4, w4], w=[a4])
                k.op("dve", lambda e: e.tensor_scalar(out=w4[:], in0=v4[:], scalar1=MAGIC, scalar2=-C2, op0=ALU.subtract, op1=ALU.mult), r=[v4], w=[w4])
                k.op("dve", lambda e: e.tensor_tensor(out=a4[:], in0=a4[:], in1=w4[:], op=ALU.add), r=[a4, w4], w=[a4])
                k.op("dve", lambda e: e.tensor_scalar(out=a4[:], in0=a4[:], scalar1=math.pi, scalar2=-math.pi, op0=ALU.min, op1=ALU.max), r=[a4], w=[a4])
                k.op("act", lambda e: e.activation(out=cs[:], in_=a4[:], func=AF.Sin), r=[a4], w=[cs])

                if l1stop < 2:
                    continue
                for blk in range(13):
                    wb = wblk[wbi[0] % 2]
                    wbi[0] += 1
                    k.dma(wb[:], w1b_d[:, :, blk * 256:(blk + 1) * 256], w=[wb])
                    for t in range(2):
                        tt = 2 * c + t
                        rr = rproj.next()

                        def mm(e, rr=rr, t=t, wb=wb):
                            ins = None
                            for kc in range(8):
                                ins = e.matmul(rr.ap, hT[:, kc, t * 128:(t + 1) * 128], wb[:, kc, :], start=(kc == 0), stop=(kc == 7))
                            return ins
                        k.op("pe", mm, r=[hT, wb], w=[rr])
                        if blk < 6:
                            k.op("act", lambda e, rr=rr, t=t, blk=blk: e.copy(qktok[:, t, blk * 256:(blk + 1) * 256], rr.ap), r=[rr], w=[qktok] + ptv)
                        elif blk < 9:
                            h0 = (blk - 6) * 4
                            k.op("act", lambda e, rr=rr, tt=tt, h0=h0: e.copy(Va[:, tt, h0:h0 + 4, 0:HD], rr.ap.rearrange("p (h d) -> p h d", h=4)), r=[rr], w=[Va])
                        else:
                            z0 = (blk - 9) * 256
                            k.op("act", lambda e, rr=rr, t=t, z0=z0: e.activation(out=zs1[:, t, z0:z0 + 256], in_=rr.ap, func=AF.Silu), r=[rr], w=[zs1])
                wb = wblk[wbi[0] % 2]
                wbi[0] += 1
                k.dma(wb[:], w1b_d[:, :, 3328:3584], w=[wb])
                for ft in range(2):
                    rr = rproj.next()

                    def mmq(e, rr=rr, ft=ft, wb=wb):
                        ins = None
                        for kc in range(8):
                            ins = e.matmul(rr.ap, wb[:, kc, ft * 128:(ft + 1) * 128], hT[:, kc, :], start=(kc == 0), stop=(kc == 7))
                        return ins
                    k.op("pe", mmq, r=[hT, wb], w=[rr])
                    k.op("act", lambda e, rr=rr, ft=ft: e.copy(mqT1[:, ft, :], rr.ap), r=[rr], w=[mqT1])

                if l1stop < 3:
                    continue
                for t in range(2):
                    tt = 2 * c + t
                    x3 = qktok[:, t, :].rearrange("p (h d) -> p h d", h=24)
                    cosb = cs[:, t, 0, :].unsqueeze(1).to_broadcast([128, 24, 32])
                    sinb = cs[:, t, 1, :].unsqueeze(1).to_broadcast([128, 24, 32])
                    accf = acc[:].rearrange("p q h d -> p (q h d)")
                    t1 = accf[:, 0:768].rearrange("p (h d) -> p h d", h=24)
                    t2 = accf[:, 768:1536].rearrange("p (h d) -> p h d", h=24)
                    k.op("dve", lambda e: e.tensor_tensor(out=t1, in0=x3[:, :, 0:32], in1=cosb, op=ALU.mult), r=[qktok, cs], w=accb)
                    k.op("dve", lambda e: e.tensor_tensor(out=t2, in0=x3[:, :, 32:64], in1=sinb, op=ALU.mult), r=[qktok, cs], w=accb)
                    k.op("dve", lambda e: e.tensor_tensor(out=qkrot[:, :, 0:32], in0=t1, in1=t2, op=ALU.subtract), r=accb, w=[qkrot])
                    k.op("dve", lambda e: e.tensor_tensor(out=t1, in0=x3[:, :, 32:64], in1=cosb, op=ALU.mult), r=[qktok, cs], w=accb)
                    k.op("dve", lambda e: e.tensor_tensor(out=t2, in0=x3[:, :, 0:32], in1=sinb, op=ALU.mult), r=[qktok, cs], w=accb)
                    k.op("dve", lambda e: e.tensor_tensor(out=qkrot[:, :, 32:64], in0=t1, in1=t2, op=ALU.add), r=accb, w=[qkrot])
                    qk2 = qkrot[:].rearrange("p h d -> p (h d)")
                    for grp in range(2):
                        def tr(e, grp=grp):
                            ins = None
                            for j in range(6):
                                cc = (grp * 6 + j) * 128
                                ins = e.transpose(PTB[:, j * 128:(j + 1) * 128], qk2[:, cc:cc + 128], identb[:])
                            return ins
                        k.op("pe", tr, r=[qkrot, identb], w=[PTB])
                        src = PTB[:, 0:768].rearrange("p (j t) -> p j t", j=6)
                        if grp == 0:
                            k.op("act", lambda e, t=t, src=src: e.copy(qT[:, :, t * 128:(t + 1) * 128], src), r=[PTB], w=[qT])
                        else:
                            k.op("dve", lambda e, tt=tt, src=src: e.tensor_copy(kT[:, :, tt * 128:(tt + 1) * 128], src), r=[PTB], w=[kT])
                ksum = smalltile("ksum", [128, 6], F32, n=2)
                k.op("dve", lambda e: e.tensor_reduce(out=ksum[:], in_=kT[:, :, c * 256:(c + 1) * 256], axis=mybir.AxisListType.X, op=ALU.add), r=[kT], w=[ksum])

                if l1stop < 4:
                    continue
                for _ in mem_attention(1, mqT1, [mqT1], 2, mtok1):
                    pass
                if l1stop < 5:
                    continue

                if c >= 1:
                    for t in range(2):
                        gpad = smalltile("gpad", [128, NH, 16], F32, n=1)
                        top8 = smalltile("top8", [128, NH, 8], F32, n=1)
                        k.op("pool", lambda e: e.memset(gpad[:], -1e30), w=[gpad])
                        gp4 = gpad[:].rearrange("p (a b) c -> p a b c", b=2)
                        for hl, rG in ((0, rGe), (1, rGo)):
                            pb = hl * 64

                            def mmg(e, t=t, rG=rG, pb=pb):
                                ins = None
                                for hp_ in range(6):
                                    ins = e.matmul(rG.ap[:, hp_ * 16:(hp_ + 1) * 16], qT[pb:pb + 64, hp_, t * 128:(t + 1) * 128], kmT[pb:pb + 64, hp_, :], start=True, stop=True)
                                return ins
                            k.op("pe", mmg, r=[qT, kmT], w=[rG])
                            k.op("dve", lambda e, rG=rG, hl=hl: e.tensor_copy(gp4[:, :, hl, 0:c], rG.ap.rearrange("p (h b) -> p h b", h=6)[:, :, 0:c]), r=[rG, gpad], w=[gpad])
                        for h in range(NH):
                            k.op("dve", lambda e, h=h: e.max(out=top8[:, h, :], in_=gpad[:, h, :]), r=[gpad], w=[top8])
                        k.op("dve", lambda e, t=t: e.tensor_tensor(out=sel[:, t, :, :], in0=gpad[:], in1=top8[:, :, 2:3].to_broadcast([128, NH, 16]), op=ALU.is_ge), r=[gpad, top8], w=[sel])
                        k.op("dve", lambda e, t=t: e.tensor_scalar(out=selst[:].rearrange("p a (b c) -> p a b c", b=2)[:, :, :, 0:16],
                                                               in0=sel[:, t, :, :].rearrange("p (a b) c -> p a b c", b=2), scalar1=-NEG, scalar2=NEG, op0=ALU.mult, op1=ALU.add),
                             r=[sel], w=[selst])

                        def trs(e):
                            ins = None
                            for j in range(6):
                                ins = e.transpose(PTB[:, j * 128:(j + 1) * 128], selst[:, j, :], identb[:])
                            return ins
                        k.op("pe", trs, r=[selst, identb], w=[PTB])
                        k.op("act", lambda e, t=t: e.copy(selT[:, :, t * 128:(t + 1) * 128], PTB[:, 0:768].rearrange("p (j t) -> p j t", j=6)), r=[PTB], w=[selT])
                k.op("act", lambda e: e.activation(out=kmT[:, :, c], in_=ksum[:], func=AF.Copy, scale=1.0 / 256.0), r=[ksum], w=[kmT])

                if l1stop < 6:
                    continue
                pti[0] = 0
                units = []
                for hp_ in range(6):
                    for b in [-1] + list(range(c)):
                        units.append((2 * hp_, b))
                        units.append((2 * hp_ + 1, b))
                stq = {}
                rvh = {}

                def issue_st(i):
                    h, b = units[i]
                    pb = (h % 2) * 64
                    hp = h // 2
                    qh = qT[pb:pb + 64, hp, :]
                    rs = rST.next()
                    if b < 0:
                        def mmo(e):
                            e.matmul(rs.ap[:, 0:256], kT[pb:pb + 64, hp, (2 * c) * 128:(2 * c + 1) * 128], qh, start=True, stop=False, skip_group_check=True)
                            e.matmul(rs.ap[:, 256:384], kT[pb:pb + 64, hp, (2 * c + 1) * 128:(2 * c + 2) * 128], qh[:, 128:256], start=False, stop=False, skip_group_check=True)
                            e.matmul(rs.ap[:, 0:128], identb[:], negmaskb[:], start=False, stop=False, skip_group_check=True)
                            return e.matmul(rs.ap[:, 256:384], identb[:], negmaskb[:], start=False, stop=True, skip_group_check=True)
                        k.op("pe", mmo, r=[kT, qT, identb, negmaskb], w=[rs])
                    else:
                        def mmp(e):
                            e.matmul(rs.ap[:, 0:256], kT[pb:pb + 64, hp, (2 * b) * 128:(2 * b + 1) * 128], qh, start=True, stop=False, skip_group_check=True)
                            e.matmul(rs.ap[:, 256:512], kT[pb:pb + 64, hp, (2 * b + 1) * 128:(2 * b + 2) * 128], qh, start=False, stop=False, skip_group_check=True)
                            return e.matmul(rs.ap[:, 0:512].rearrange("p (a b) -> p a b", a=2), Eoh[pb:pb + 16, b, :],
                                            selT[pb:pb + 16, hp, :].unsqueeze(1).to_broadcast([16, 2, 256]), start=False, stop=True, skip_group_check=True)
                        k.op("pe", mmp, r=[kT, qT, Eoh, selT], w=[rs])
                    stq[i] = rs

                def issue_rest(i):
                    h, b = units[i]
                    rs = stq.pop(i)
                    pt = ptv[pti[0] % NPT]
                    first_use = pti[0] < NPT
                    pti[0] += 1
                    ab = accb[h]
                    ptw = [pt, qktok] if first_use else [pt]
                    last = (b == c - 1) or (c == 0)
                    if b < 0:
                        k.op("act", lambda e: e.activation(out=pt[:, 0:384], in_=rs.ap[:, 0:384], func=AF.Exp, scale=0.125), r=[rs], w=ptw)
                        rv = rPV.next()
                        rvh[h] = rv

                        def mmpo(e):
                            e.matmul(rv.ap[:, 0:65], pt[:, 0:128], Va[:, 2 * c, h, :], start=True, stop=False, skip_group_check=True)
                            e.matmul(rv.ap[:, 65:130], pt[:, 128:256], Va[:, 2 * c, h, :], start=False, stop=False, skip_group_check=True)
                            return e.matmul(rv.ap[:, 65:130], pt[:, 256:384], Va[:, 2 * c + 1, h, :], start=False, stop=last, skip_group_check=True)
                        k.op("pe", mmpo, r=[pt, Va], w=[rv])
                        if last:
                            k.op("act", lambda e: e.copy(acc[:, :, h, :], rv.ap.rearrange("p (q d) -> p q d", q=2)), r=[rv], w=[ab])
                    else:
                        k.op("act", lambda e: e.activation(out=pt[:, :], in_=rs.ap, func=AF.Exp, scale=0.125), r=[rs], w=ptw)
                        rv = rvh[h]

                        def mmpv(e):
                            e.matmul(rv.ap[:, 0:65], pt[:, 0:128], Va[:, 2 * b, h, :], start=False, stop=False, skip_group_check=True)
                            e.matmul(rv.ap[:, 65:130], pt[:, 128:256], Va[:, 2 * b, h, :], start=False, stop=False, skip_group_check=True)
                            e.matmul(rv.ap[:, 0:65], pt[:, 256:384], Va[:, 2 * b + 1, h, :], start=False, stop=False, skip_group_check=True)
                            return e.matmul(rv.ap[:, 65:130], pt[:, 384:512], Va[:, 2 * b + 1, h, :], start=False, stop=last, skip_group_check=True)
                        k.op("pe", mmpv, r=[pt, Va], w=[rv])
                        if last:
                            k.op("act", lambda e: e.copy(acc[:, :, h, :], rv.ap.rearrange("p (q d) -> p q d", q=2)), r=[rv], w=[ab])
                nd = len(units) // 2
                issue_st(0)
                issue_st(1)
                for d in range(nd):
                    if d + 1 < nd:
                        issue_st(2 * d + 2)
                        issue_st(2 * d + 3)
                    issue_rest(2 * d)
                    issue_rest(2 * d + 1)
                if l1stop < 7:
                    continue
                rden = smalltile("rden1", [128, 2, NH], F32, n=1)
                k.op("dve", lambda e: e.reciprocal(rden[:], acc[:, :, :, HD]), r=accb, w=[rden])
                k.op("dve", lambda e: e.tensor_tensor(out=acc[:, :, :, 0:HD], in0=acc[:, :, :, 0:HD], in1=rden[:].unsqueeze(3).to_broadcast([128, 2, NH, HD]), op=ALU.mult), r=accb + [rden], w=accb)
                for t in range(2):
                    tt = 2 * c + t
                    k.op("dve", lambda e, t=t: e.tensor_tensor(out=ybf1[:, 0:768].rearrange("p (h d) -> p h d", h=NH), in0=acc[:, t, :, 0:HD],
                                                               in1=zs1[:, t, 0:768].rearrange("p (h d) -> p h d", h=NH), op=ALU.mult), r=accb + [zs1], w=[ybf1])
                    k.op("pool", lambda e, t=t: e.tensor_tensor(out=ybf1[:, 768:1024], in0=mtok1[:, t, :, :].rearrange("p h d -> p (h d)"), in1=zs1[:, t, 768:1024], op=ALU.mult), r=[mtok1, zs1], w=[ybf1])
                    out_proj(ybf1[:], [ybf1], wout1b, htok[:, t, :], [htok], htok[:, t, :], [htok])
                    ss = smalltile("fn_ss", [128, 1])
                    k.op("act", lambda e, t=t: e.activation(out=junk["ap"], in_=htok[:, t, :], func=AF.Square, accum_out=ss[:]), r=[htok], w=junk["bufs"] + [ss])
                    k.op("act", lambda e: e.activation(out=ss[:], in_=ss[:], func=AF.Ln, scale=1.0 / D, bias=EPS), r=[ss], w=[ss])
                    k.op("act", lambda e: e.activation(out=ss[:], in_=ss[:], func=AF.Exp, scale=-0.5), r=[ss], w=[ss])
                    k.op("dve", lambda e, t=t: e.scalar_tensor_tensor(out=htok[:, t, :], in0=htok[:, t, :], scalar=ss[:], in1=fnorm_bc[:], op0=ALU.mult, op1=ALU.mult), r=[htok, ss, fnorm_bc], w=[htok])
                    k.dma(out_d[tt * 128:(tt + 1) * 128, :], htok[:, t, :], r=[htok], is_output=True)
            k.barrier()
            es1.close()
            k.es = es

        k.finish()
    return nc


_CACHE = {}


def kernel(**inputs):
    S = inputs["x"].shape[1]
    B = inputs["x"].shape[0]
    key = (S,)
    if key not in _CACHE:
        _CACHE[key] = build_program(S)
    nc = _CACHE[key]
    in_maps = []
    for b in range(B):
        m = {}
        for name, v in inputs.items():
            a = np.asarray(v)
            if name in ("x", "mem", "positions"):
                a = a[b]
            m[name] = np.ascontiguousarray(a)
        in_maps.append(m)
    res = run_bass_kernel_spmd(nc, in_maps, core_ids=list(range(B)))
    return np.stack([r["out"] for r in res.results], axis=0)
```
